# Optimizing a Trainium2 kernel written in Bass

```python
import math
import jax, jax.numpy as jnp
from jax import lax
import numpy as np

D_MODEL = 1024
BATCH = 8
SEQ = 2048
DEPTH = 4

HEAD_DIM = 64
N_MOBA_HEADS = D_MODEL // (2 * HEAD_DIM)
N_FOX_HEADS = D_MODEL // (2 * HEAD_DIM)
MOBA_WIDTH = N_MOBA_HEADS * HEAD_DIM
FOX_WIDTH = N_FOX_HEADS * HEAD_DIM
MIX_WIDTH = MOBA_WIDTH + FOX_WIDTH
IN_WIDTH = 3 * MIX_WIDTH + N_FOX_HEADS
MOBA_BLOCK = 256
MOBA_TOPK = 3
MOBA_Q_CHUNK = 32
FOX_Q_BLOCK = 128
NUM_BUCKETS = 32
MAX_DISTANCE = 1024
D_FF = -(-(8 * D_MODEL) // (3 * 256)) * 256
RMS_EPS = 1e-6

kernel_name = "hymba_style_moba_fox_hybrid"


def rmsnorm(x, g):
    xf = x.astype(jnp.float32)
    y = xf * lax.rsqrt(jnp.mean(xf * xf, axis=-1, keepdims=True) + RMS_EPS)
    return (y * g.astype(jnp.float32)).astype(x.dtype)


def t5_bucket(dist):
    max_exact = NUM_BUCKETS // 2
    is_small = dist < max_exact
    d = jnp.maximum(dist, 1).astype(jnp.float32)
    large = max_exact + (jnp.log(d / max_exact) / math.log(MAX_DISTANCE / max_exact)
                         * (NUM_BUCKETS - max_exact)).astype(jnp.int32)
    large = jnp.minimum(large, NUM_BUCKETS - 1)
    return jnp.where(is_small, dist, large)


def split_heads(t, n_heads):
    b, s, _ = t.shape
    return t.reshape(b, s, n_heads, HEAD_DIM).transpose(0, 2, 1, 3)


def moba_attention(q, k, v, rel_bias):
    B, H, S, Dh = q.shape
    nb = -(-S // MOBA_BLOCK)
    pad = nb * MOBA_BLOCK - S
    kp = jnp.pad(k, ((0, 0), (0, 0), (0, pad), (0, 0)))
    vp = jnp.pad(v, ((0, 0), (0, 0), (0, pad), (0, 0)))
    kb = kp.reshape(B, H, nb, MOBA_BLOCK, Dh)
    vb = vp.reshape(B, H, nb, MOBA_BLOCK, Dh)
    kmean = jnp.mean(kb.astype(jnp.float32), axis=3)
    qblk = jnp.arange(S) // MOBA_BLOCK
    gate = jnp.einsum('bhsd,bhnd->bhsn', q.astype(jnp.float32), kmean)
    past = jnp.arange(nb)[None, :] < qblk[:, None]
    gate = jnp.where(past[None, None], gate, -jnp.inf)
    topk = min(MOBA_TOPK, nb)
    _, gidx = lax.top_k(gate, topk)
    scale = HEAD_DIM ** -0.5
    table_t = rel_bias.T
    bi = jnp.arange(B)[:, None, None, None]
    hi = jnp.arange(H)[None, :, None, None]
    n_chunks = S // MOBA_Q_CHUNK

    def chunk(c):
        t0 = c * MOBA_Q_CHUNK
        cur = t0 // MOBA_BLOCK
        tq = t0 + jnp.arange(MOBA_Q_CHUNK)
        qc = lax.dynamic_slice_in_dim(q, t0, MOBA_Q_CHUNK, axis=2)
        ic = lax.dynamic_slice_in_dim(gidx, t0, MOBA_Q_CHUNK, axis=2)
        ksel = kb[bi, hi, ic]
        vsel = vb[bi, hi, ic]
        s_sel = jnp.einsum('bhqd,bhqkld->bhqkl', qc, ksel).astype(jnp.float32) * scale
        kpos = ic[..., None] * MOBA_BLOCK + jnp.arange(MOBA_BLOCK)
        dist = jnp.maximum(tq[None, None, :, None, None] - kpos, 0)
        s_sel = s_sel + table_t[hi[..., None], t5_bucket(dist)].astype(jnp.float32)
        valid = (jnp.arange(topk) < cur)[None, None, None, :, None]
        s_sel = jnp.where(valid, s_sel, -jnp.inf)
        s_sel = s_sel.reshape(B, H, MOBA_Q_CHUNK, topk * MOBA_BLOCK)
        kcur = lax.dynamic_slice_in_dim(kp, cur * MOBA_BLOCK, MOBA_BLOCK, axis=2)
        vcur = lax.dynamic_slice_in_dim(vp, cur * MOBA_BLOCK, MOBA_BLOCK, axis=2)
        s_own = jnp.einsum('bhqd,bhld->bhql', qc, kcur).astype(jnp.float32) * scale
        kpos_own = cur * MOBA_BLOCK + jnp.arange(MOBA_BLOCK)
        d_own = tq[:, None] - kpos_own[None, :]
        b_own = rel_bias[t5_bucket(jnp.maximum(d_own, 0))].transpose(2, 0, 1)
        s_own = s_own + b_own[None].astype(jnp.float32)
        s_own = jnp.where((d_own >= 0)[None, None], s_own, -jnp.inf)
        p = jax.nn.softmax(jnp.concatenate([s_sel, s_own], axis=-1), axis=-1).astype(v.dtype)
        p_sel = p[..., :topk * MOBA_BLOCK].reshape(B, H, MOBA_Q_CHUNK, topk, MOBA_BLOCK)
        p_own = p[..., topk * MOBA_BLOCK:]
        return (jnp.einsum('bhqkl,bhqkld->bhqd', p_sel, vsel)
                + jnp.einsum('bhql,bhld->bhqd', p_own, vcur))

    out = lax.map(chunk, jnp.arange(n_chunks))
    return out.transpose(1, 2, 0, 3, 4).reshape(B, H, S, Dh)


def fox_attention(q, k, v, log_f):
    B, H, S, Dh = q.shape
    c = jnp.cumsum(log_f, axis=-1)
    scale = HEAD_DIM ** -0.5
    kpos = jnp.arange(S)

    def block(i):
        t0 = i * FOX_Q_BLOCK
        qi = lax.dynamic_slice_in_dim(q, t0, FOX_Q_BLOCK, axis=2)
        ci = lax.dynamic_slice_in_dim(c, t0, FOX_Q_BLOCK, axis=2)
        s = (jnp.einsum('bhqd,bhkd->bhqk', qi, k).astype(jnp.float32) * scale
             + ci[..., :, None] - c[:, :, None, :])
        tq = t0 + jnp.arange(FOX_Q_BLOCK)
        s = jnp.where((kpos[None, :] <= tq[:, None])[None, None], s, -jnp.inf)
        p = jax.nn.softmax(s, axis=-1).astype(v.dtype)
        return jnp.einsum('bhqk,bhkd->bhqd', p, v)

    out = lax.map(block, jnp.arange(S // FOX_Q_BLOCK))
    return out.transpose(1, 2, 0, 3, 4).reshape(B, H, S, Dh)


def setup_inputs(seed: int = 0) -> dict:
    key = jax.random.key(seed)
    ks = jax.random.split(key, 11)
    f32 = jnp.float32
    x = jax.random.normal(ks[0], (BATCH, SEQ, D_MODEL), f32)
    w_in = jax.random.normal(ks[1], (DEPTH, D_MODEL, IN_WIDTH), f32) * D_MODEL ** -0.5
    b_f = jax.random.uniform(ks[2], (DEPTH, N_FOX_HEADS), f32, minval=1.0, maxval=4.0)
    w_o = jax.random.normal(ks[3], (DEPTH, MIX_WIDTH, D_MODEL), f32) * MIX_WIDTH ** -0.5
    g_attn = 1.0 + 0.05 * jax.random.normal(ks[4], (DEPTH, D_MODEL), f32)
    w_gu = jax.random.normal(ks[5], (DEPTH, D_MODEL, 2 * D_FF), f32) * D_MODEL ** -0.5
    w_down = jax.random.normal(ks[6], (DEPTH, D_FF, D_MODEL), f32) * D_FF ** -0.5
    g_ffn = 1.0 + 0.05 * jax.random.normal(ks[7], (DEPTH, D_MODEL), f32)
    rel_bias = 0.5 * jax.random.normal(ks[8], (NUM_BUCKETS, N_MOBA_HEADS), f32)
    g_final = 1.0 + 0.05 * jax.random.normal(ks[9], (D_MODEL,), f32)
    return {"x": x, "w_in": w_in, "b_f": b_f, "w_o": w_o, "g_attn": g_attn,
            "w_gu": w_gu, "w_down": w_down, "g_ffn": g_ffn,
            "rel_bias": rel_bias, "g_final": g_final}


def reference(x, w_in, b_f, w_o, g_attn, w_gu, w_down, g_ffn, rel_bias, g_final):
    M, F = MOBA_WIDTH, FOX_WIDTH
    for layer in range(DEPTH):
        h = rmsnorm(x, g_attn[layer])
        proj = jnp.einsum('bsd,de->bse', h, w_in[layer])
        q_m = split_heads(proj[..., 0:M], N_MOBA_HEADS)
        k_m = split_heads(proj[..., M:2 * M], N_MOBA_HEADS)
        v_m = split_heads(proj[..., 2 * M:3 * M], N_MOBA_HEADS)
        o0 = 3 * M
        q_f = split_heads(proj[..., o0:o0 + F], N_FOX_HEADS)
        k_f = split_heads(proj[..., o0 + F:o0 + 2 * F], N_FOX_HEADS)
        v_f = split_heads(proj[..., o0 + 2 * F:o0 + 3 * F], N_FOX_HEADS)
        f_logit = proj[..., o0 + 3 * F:].astype(jnp.float32) + b_f[layer].astype(jnp.float32)
        log_f = jax.nn.log_sigmoid(f_logit).transpose(0, 2, 1)
        y_m = moba_attention(q_m, k_m, v_m, rel_bias)
        y_f = fox_attention(q_f, k_f, v_f, log_f)
        y = jnp.concatenate([y_m, y_f], axis=1)
        B_, H_, S_, _ = y.shape
        y = y.transpose(0, 2, 1, 3).reshape(B_, S_, H_ * HEAD_DIM)
        x = x + jnp.einsum('bse,ed->bsd', y, w_o[layer])
        h = rmsnorm(x, g_ffn[layer])
        gu = jnp.einsum('bsd,df->bsf', h, w_gu[layer])
        x = x + jnp.einsum('bsf,fd->bsd', jax.nn.silu(gu[..., :D_FF]) * gu[..., D_FF:], w_down[layer])
    return rmsnorm(x, g_final)
```

```python
import contextlib
import math
import numpy as np
import concourse.bass as bass
import concourse.mybir as mybir
from concourse.bass_utils import run_bass_kernel_spmd

F32 = mybir.dt.float32
BF16 = mybir.dt.bfloat16
U8 = mybir.dt.uint8
AF = mybir.ActivationFunctionType
ALU = mybir.AluOpType
AX = mybir.AxisListType

S = 2048
D = 1024
NT = 16
L = 4
DFF = 2816
NFC = 22
BIG = 30000.0
EPS = 1e-6
EXTW = 2176
N_CORES = 8


class Buf:
    __slots__ = ("name", "w", "r", "dsem", "dcnt")

    def __init__(self, name):
        self.name = name
        self.w = []
        self.r = []
        self.dsem = None
        self.dcnt = 0


class Sched:
    ENG = ("pe", "act", "dve", "pool", "sp")

    def __init__(self, sems):
        self.ops = {e: [] for e in self.ENG}
        self.cnt = {e: 0 for e in self.ENG}
        self.waited = {e: {} for e in self.ENG}
        self.esem = {e: sems[i] for i, e in enumerate(self.ENG)}
        self.free_sems = list(sems[len(self.ENG):])
        self.semh = {}
        for e in self.ENG:
            self.semh[e] = self.esem[e]

    def _need(self, eng, t):
        key, val = t
        if key == "pe" and eng == "pe":
            return
        w = self.waited[eng]
        if w.get(key, 0) >= val:
            return
        w[key] = val
        self.ops[eng].append(("wait", key, val))

    @staticmethod
    def _addr(b, tk):
        for i, (k, v) in enumerate(b.r):
            if k == tk[0]:
                if v < tk[1]:
                    b.r[i] = tk
                return
        b.r.append(tk)

    def _deps(self, eng, reads, writes, extra):
        for t in extra:
            self._need(eng, t)
        for b in reads:
            for t in b.w:
                self._need(eng, t)
        for b in writes:
            for t in b.w:
                self._need(eng, t)
            for t in b.r:
                self._need(eng, t)

    def _finish(self, tk, reads, writes):
        for b in writes:
            b.w = [tk]
            b.r = []
        for b in reads:
            if b not in writes:
                self._addr(b, tk)

    def op(self, eng, fn, reads=(), writes=(), extra=()):
        self._deps(eng, reads, writes, extra)
        self.cnt[eng] += 1
        tk = (eng, self.cnt[eng])
        self.ops[eng].append(("op", fn, True))
        self._finish(tk, reads, writes)
        return tk

    def pe_group(self, fns, reads=(), writes=()):
        self._deps("pe", reads, writes, ())
        self.cnt["pe"] += 1
        tk = ("pe", self.cnt["pe"])
        for i, fn in enumerate(fns):
            self.ops["pe"].append(("op", fn, i == len(fns) - 1))
        self._finish(tk, reads, writes)
        return tk

    def pe_open(self, reads=(), writes=()):
        self._deps("pe", reads, writes, ())
        return (list(reads), list(writes))

    def pe_part(self, fn):
        self.ops["pe"].append(("op", fn, False))

    def pe_close(self, handle, fn):
        reads, writes = handle
        self.cnt["pe"] += 1
        tk = ("pe", self.cnt["pe"])
        self.ops["pe"].append(("op", fn, True))
        self._finish(tk, reads, writes)
        return tk

    def dma(self, q, out_ap, in_ap, dst, reads=(), also_writes=(), **kw):
        writes = [dst] + list(also_writes)
        self._deps(q, reads, writes, ())
        if dst.dsem is None:
            dst.dsem = self.free_sems.pop()
            self.semh[("d", id(dst))] = dst.dsem
        dst.dcnt += 16
        tk = (("d", id(dst)), dst.dcnt)
        self.ops[q].append(("dma", out_ap, in_ap, ("d", id(dst)), kw))
        self._finish(tk, reads, writes)
        return tk

    def wait_all(self, eng, bufs):
        for b in bufs:
            for t in b.w:
                self._need(eng, t)

    def run(self, eng, e):
        semh = self.semh
        for o in self.ops[eng]:
            if o[0] == "wait":
                e.wait_ge(semh[o[1]], o[2])
            elif o[0] == "op":
                ins = o[1](e)
                if o[2]:
                    ins.then_inc(semh[eng], 1)
            else:
                e.dma_start(out=o[1], in_=o[2], **o[4]).then_inc(semh[o[3]], 16)


def build_program(n_layers=L, stop_after=None):
    nc = bass.Bass("TRN2", target_bir_lowering=False)
    dbg = stop_after is not None

    def din(name, shape):
        return nc.dram_tensor(name, list(shape), F32, kind="ExternalInput").ap()

    x_d = din("x", [S, D])
    wqk_d = din("wqk", [L * 16, 128, 1024])
    wv_d = din("wv", [L * 8, 128, 1024])
    wf_d = din("wf", [L, 128, 64])
    wo_d = din("wo", [L, 128, 8192])
    wg_d = din("wg", [L * NFC, 128, 1024])
    wu_d = din("wu", [L * NFC, 128, 1024])
    wd_d = din("wd", [L * NFC, 128, 1024])
    gall_d = din("gall", [9, D])
    bf_d = din("bfl", [1, 32])
    rb_d = din("rb", [32, 8])
    cst_d = din("cst", [128, 512])
    ident_d = din("ident", [128, 128])
    kaug_d = din("kaug", [2, 8, S])
    ohb_d = din("ohb", [33, EXTW])
    out_d = nc.dram_tensor("out", [S, D], F32, kind="ExternalOutput").ap()
    if dbg:
        dbg_d = nc.dram_tensor("dbg", [128, 16384], F32, kind="ExternalOutput").ap()
    ext_t = nc.dram_tensor("ext_scr", [8, EXTW], BF16)
    ts_t = nc.dram_tensor("ts_scr", [8, 128, S], BF16)
    cpd_t = nc.dram_tensor("cpd_scr", [8, 3, S], BF16)
    ext_d = ext_t.ap()
    ts_d = ts_t.ap()
    cpd_d = cpd_t.ap()
    sum_d = nc.dram_tensor("sum_scr", [2, 512], F32).ap()

    x_v = x_d.rearrange("(i p) d -> i p d", p=128)
    out_v = out_d.rearrange("(i p) d -> i p d", p=128)

    ARENA_BYTES = 81344
    NSEM = 100
    with contextlib.ExitStack() as es:
        Xt = es.enter_context(nc.sbuf_tensor("X", [128, NT * D], F32))
        HTt = es.enter_context(nc.sbuf_tensor("HT", [128, 8 * S], BF16))
        YTt = es.enter_context(nc.sbuf_tensor("YT", [128, 8 * S], BF16))
        ARt = es.enter_context(nc.sbuf_tensor("AR", [128, ARENA_BYTES], U8))
        PSt = es.enter_context(nc.psum_tensor("PS", [128, 4096], F32))
        sems = [es.enter_context(nc.semaphore("s%d" % i)) for i in range(NSEM)]
        block = es.enter_context(nc.Block())
        sc = Sched(sems)

        X = Xt[:].rearrange("p (i d) -> p i d", i=NT)
        HT = HTt[:].rearrange("p (c t) -> p c t", c=8)
        YT = YTt[:].rearrange("p (c t) -> p c t", c=8)

        apos = [0]

        def carve(nbytes, dtype):
            off = apos[0]
            apos[0] += (nbytes + 63) // 64 * 64
            assert apos[0] <= ARENA_BYTES, apos[0]
            return ARt[:, off:off + nbytes].bitcast(dtype)

        QT = [None, None]
        KT = [None, None]
        QT[0] = carve(4096, BF16)
        KT[0] = carve(4096, BF16)
        STRIP = carve(8192, F32)
        WOv = ARt[:, 0:16384].bitcast(BF16)
        STG = [ARt[:, 12288 + 1024 * k_:12288 + 1024 * (k_ + 1)].bitcast(BF16) for k_ in range(4)]
        STGI = [0]
        QT[1] = carve(4096, BF16)
        KT[1] = carve(4096, BF16)
        TMPr = [carve(2048, F32), carve(2048, F32)]
        WDv = [ARt[:, 16384 + 2048 * s_:16384 + 2048 * (s_ + 1)].bitcast(BF16) for s_ in range(6)]
        PT = [carve(2048, BF16) for _ in range(3)]
        off_OSB = apos[0]
        OSB = carve(2048, F32)
        BC = carve(2048, F32)
        Gn = ARt[:, off_OSB:off_OSB + 4096].bitcast(F32)
        OSB2 = [OSB, carve(2048, F32)]
        BC2 = [BC, carve(2048, F32)]
        VA = [carve(NT * 65 * 2, BF16).rearrange("p (j c) -> p j c", c=65) for _ in range(2)]
        VB = [carve(NT * 128 * 2, BF16).rearrange("p (j c) -> p j c", c=128) for _ in range(2)]
        WQK = carve(2048, BF16)
        WV = carve(2048, BF16)
        WG = [carve(2048, BF16) for _ in range(2)]
        WU = [carve(2048, BF16) for _ in range(2)]
        WF = carve(128, BF16)
        IDENT = carve(256, BF16)
        CST = carve(2048, F32)
        ONES = carve(512, F32)
        CM = carve(256, BF16)
        BFT = carve(128, F32)
        SS = carve(64, F32)
        RSTD = carve(64, F32)
        RBA = carve(32, F32)
        GT = [carve(512, F32) for _ in range(4)]
        MX = carve(64, F32)
        MVB = carve(256, BF16)
        KM = carve(32, F32)
        KMH = carve(16, BF16)
        KML = carve(16, BF16)
        KMR = carve(32, F32)
        FL = carve(512, F32)
        LOGF = carve(512, F32)
        PSX = carve(512, F32)
        NEGC = carve(512, F32)
        CR = carve(2048, F32)
        CPS = carve(3072, BF16)
        ESTRIP = STRIP.bitcast(BF16)[:, 0:S]
        JB = carve(256, BF16)
        Jm = CST[:, 0:128]
        Um = CST[:, 128:256]
        PM = CST[:, 256:384]
        NOTOWN = CST[:, 384:512]

        def psb(b, dtype=F32):
            v = PSt[:, 512 * b:512 * (b + 1)]
            return v if dtype == F32 else v.bitcast(dtype)

        Xb = [[Buf("X%d_%d" % (i, h)) for h in range(2)] for i in range(NT)]
        HTb = [Buf("HT%d" % i) for i in range(NT)]
        YTb = [[Buf("YT%d_%d" % (c, g)) for g in range(4)] for c in range(8)]
        PSb = [Buf("PS%d" % b) for b in range(8)]
        QTd = [Buf("QTd0"), Buf("QTd1")]
        QTa = [Buf("QTa0"), Buf("QTa1")]
        KTd = [Buf("KTd0"), Buf("KTd1")]
        KTa = [Buf("KTa0"), Buf("KTa1")]
        STRIPb = Buf("STRIP")
        TMPb = [Buf("TMP0"), Buf("TMP1")]
        PTb = [Buf("PT%d" % i) for i in range(3)]
        OSBb = Buf("OSB")
        BCb = Buf("BC")
        OSB2b = [OSBb, Buf("OSB1")]
        BC2b = [BCb, Buf("BC1")]
        SUMb = [Buf("SUM0"), Buf("SUM1")]
        Vb = [Buf("Vm"), Buf("Vf")]
        WQKb = Buf("WQK")
        WVb = Buf("WV")
        WGb = [Buf("WG0"), Buf("WG1")]
        WUb = [Buf("WU0"), Buf("WU1")]
        WFb = Buf("WF")
        WOb = Buf("WO")
        WDb = [Buf("WD%d" % i) for i in range(6)]
        IDENTb = Buf("IDENT")
        CSTb = Buf("CST")
        ONESb = Buf("ONES")
        CMb = Buf("CM")
        BFTb = Buf("BFT")
        SSb = Buf("SS")
        SSt = [Buf("SSt%d" % i) for i in range(NT)]
        RSTDt = [Buf("RSTDt%d" % i) for i in range(NT)]
        RSTDb = Buf("RSTD")
        RBAb = Buf("RBA")
        GTb = [Buf("GT%d" % i) for i in range(4)]
        MXb = Buf("MX")
        MVBb = Buf("MVB")
        KMb = Buf("KM")
        KMHb = Buf("KMH")
        FLb = Buf("FL")
        LOGFb = Buf("LOGF")
        PSXb = Buf("PSX")
        NEGCb = Buf("NEGC")
        CRb = Buf("CR")
        CPSb = Buf("CPS")
        EXTb = Buf("EXT")
        TSb = [Buf("TS%d" % h) for h in range(8)]
        CPDb = Buf("CPD")
        OUTb = [Buf("OUT%d" % i) for i in range(4)]
        DBGb = Buf("DBG")
        STGb = [Buf("STG%d" % k_) for k_ in range(4)]
        WO_OVER = [QTd[0], QTa[0], KTd[0], KTa[0], STRIPb] + STGb
        WD_OVER = [[QTd[1], QTa[1]], [QTd[1], QTa[1]], [KTd[1], KTa[1]], [KTd[1], KTa[1]],
                   [TMPb[0]], [TMPb[1]]]

        sc.dma("sp", CST, cst_d, CSTb)
        sc.dma("pool", IDENT, ident_d, IDENTb)
        sc.dma("sp", BFT[:, 0:32], bf_d.to_broadcast([128, 32]), BFTb)
        for i in range(NT):
            sc.dma("sp", X[:, i, :], x_v[i], Xb[i][0], also_writes=[Xb[i][1]])
        sc.op("dve", lambda e: e.memset(ONES, 1.0), writes=[ONESb])
        sc.op("dve", lambda e: e.tensor_scalar(CM, Um, -1.0, BIG, ALU.add, ALU.mult),
              reads=[CSTb], writes=[CMb])
        for t in range(2):
            sc.op("dve", lambda e, t=t: e.memset(QT[t][64:72, :], 0.0), writes=[QTa[t]])
            sc.op("dve", lambda e, t=t: e.memset(VB[t], 0.0), writes=[Vb[t]])
            sc.op("dve", lambda e, t=t: e.memset(VB[t][:, :, 0:1], 1.0), writes=[Vb[t]])
            sc.op("dve", lambda e, t=t: e.memset(VA[t][:, :, 64:65], 1.0), writes=[Vb[t]])
        sc.op("dve", lambda e: e.memset(PSX, 0.0), writes=[PSXb])

        def t5_prologue():
            sc.dma("sp", RBA[0:32, 0:8], rb_d, RBAb)
            sc.op("dve", lambda e: e.memset(RBA[32:33, 0:8], 1.0), writes=[RBAb])
            sc.op("dve", lambda e: e.tensor_copy(out=JB, in_=Jm), reads=[CSTb], writes=[CSTb])
            osbb = OSB.bitcast(BF16)
            for ch in range(5):
                c0 = 512 * ch
                w = min(512, EXTW - c0)
                sc.dma("sp", TMPr[ch % 2][0:33, 0:w], ohb_d[:, c0:c0 + w], TMPb[ch % 2])
                sc.pe_group([lambda e, w=w, ch=ch: e.matmul(psb(7)[0:8, 0:w], lhsT=RBA[0:33, 0:8],
                                                            rhs=TMPr[ch % 2][0:33, 0:w], start=True, stop=True)],
                            reads=[RBAb, TMPb[ch % 2]], writes=[PSb[7]])
                sc.op("act", lambda e, w=w: e.activation(out=osbb[0:8, 0:w], in_=psb(7)[0:8, 0:w], func=AF.Exp),
                      reads=[PSb[7]], writes=[OSBb])
                sc.dma("sp", ext_d[:, c0:c0 + w], osbb[0:8, 0:w], EXTb, reads=[OSBb])
            TSTs = [ARt[:, 16384:16384 + 4096].bitcast(BF16), ARt[:, 20480:20480 + 4096].bitcast(BF16),
                    ARt[:, 0:4096].bitcast(BF16), ARt[:, 4096:8192].bitcast(BF16)]
            TSTbs = [[QTd[1], QTa[1]], [KTd[1], KTa[1]], [QTd[0], QTa[0]], [KTd[0], KTa[0]]]
            for h in range(8):
                TST, TSTb = TSTs[h % 4], TSTbs[h % 4]
                tap = bass.AP(ext_t, h * EXTW, [[1, 128], [1, S]])
                sc.dma("sp", TST, tap, TSTb[0], reads=[EXTb], also_writes=TSTb[1:])
                for ch in range(4):
                    k = ch % 2
                    eb = TMPr[k].bitcast(BF16)[:, 0:512]
                    sc.pe_group([lambda e, ch=ch, TST=TST: e.matmul(psb(5 + ch % 2), lhsT=JB, rhs=TST[:, 512 * ch:512 * ch + 512],
                                                            start=True, stop=True)],
                                reads=[CSTb] + TSTb, writes=[PSb[5 + ch % 2]])
                    if ch % 2 == 0:
                        sc.op("act", lambda e, ch=ch, eb=eb: e.activation(out=eb, in_=psb(5 + ch % 2), func=AF.Copy),
                              reads=[PSb[5 + ch % 2]], writes=[TMPb[k]])
                    else:
                        sc.op("dve", lambda e, ch=ch, eb=eb: e.tensor_copy(out=eb, in_=psb(5 + ch % 2)),
                              reads=[PSb[5 + ch % 2]], writes=[TMPb[k]])
                    sc.dma("sp", ts_d[h, :, 512 * ch:512 * ch + 512], eb, TSb[h], reads=[TMPb[k]])
            for t in range(2):
                sc.op("dve", lambda e, t=t: e.memset(QT[t][64:72, :], 0.0), writes=[QTa[t]])


        Gv = STRIP[:, 0:D]

        def ss_begin():
            sc.op("dve", lambda e: e.memset(SS[:, 0:NT], 0.0), writes=[SSb])

        def ss_tile(i):
            junk = OSB.bitcast(BF16)
            sc.op("act", lambda e, i=i: e.activation(out=junk, in_=X[:, i, :], func=AF.Square,
                                                     accum_out=SS[:, i:i + 1]),
                  reads=[Xb[i][0], Xb[i][1]], writes=[OSBb, SSb])

        def emit_norm(gidx, final=False, ss_done=False, hook=None):
            sc.dma("sp", Gv, gall_d[gidx:gidx + 1, :].to_broadcast([128, D]), STRIPb)
            if not ss_done:
                ss_begin()
                for i in range(NT):
                    ss_tile(i)
            sc.op("dve", lambda e: e.tensor_scalar(RSTD[:, 0:NT], SS[:, 0:NT], 1.0 / D, EPS, ALU.mult, ALU.add),
                  reads=[SSb], writes=[RSTDb])
            sc.op("act", lambda e: e.activation(out=RSTD[:, 0:NT], in_=RSTD[:, 0:NT], func=AF.Sqrt),
                  reads=[], writes=[RSTDb])
            sc.op("dve", lambda e: e.reciprocal(RSTD[:, 0:NT], RSTD[:, 0:NT]),
                  reads=[], writes=[RSTDb])
            for i in range(NT):
                if final:
                    for hf in range(2):
                        sc.op("dve", lambda e, i=i, hf=hf: e.scalar_tensor_tensor(
                            out=X[:, i, 512 * hf:512 * hf + 512], in0=X[:, i, 512 * hf:512 * hf + 512],
                            scalar=RSTD[:, i:i + 1], in1=Gv[:, 512 * hf:512 * hf + 512],
                            op0=ALU.mult, op1=ALU.mult),
                            reads=[RSTDb, STRIPb], writes=[Xb[i][hf]])
                    sc.dma("sp", out_v[i], X[:, i, :], OUTb[i % 4], reads=[Xb[i][0], Xb[i][1]])
                    continue
                k = i % 2
                hn = TMPr[k].bitcast(BF16)
                sc.op("dve", lambda e, i=i, hn=hn: e.scalar_tensor_tensor(
                    out=hn, in0=X[:, i, :], scalar=RSTD[:, i:i + 1], in1=Gv, op0=ALU.mult, op1=ALU.mult),
                    reads=[Xb[i][0], Xb[i][1], RSTDb, STRIPb], writes=[TMPb[k]])
                bank = i % 2
                pst = psb(bank, BF16).rearrange("p (c t) -> p c t", c=8)
                sc.pe_group([lambda e, c=c, hn=hn, pst=pst: e.transpose(pst[:, c, :], hn[:, 128 * c:128 * c + 128], IDENT)
                             for c in range(8)],
                            reads=[TMPb[k], IDENTb], writes=[PSb[bank]])
                sc.op("act", lambda e, i=i, pst=pst: e.activation(out=HT[:, :, 128 * i:128 * i + 128], in_=pst, func=AF.Copy),
                      reads=[PSb[bank]], writes=[HTb[i]])
                if hook is not None:
                    hook(i)

        def make_norm(gidx, final):
            junk = OSB2[1].bitcast(BF16)
            gb = [OSB2b[0], BC2b[0]]

            def begin():
                sc.dma("sp", Gn, gall_d[gidx:gidx + 1, :].to_broadcast([128, D]), gb[0], also_writes=gb[1:])
                sc.op("dve", lambda e: e.memset(SS[:, 0:NT], 0.0), writes=SSt)

            def stA(i):
                sc.op("act", lambda e: e.activation(out=junk, in_=X[:, i, :], func=AF.Square, accum_out=SS[:, i:i + 1]),
                      reads=[Xb[i][0], Xb[i][1]], writes=[OSB2b[1], SSt[i]])

            def stB(i):
                sc.op("dve", lambda e: e.tensor_scalar(RSTD[:, i:i + 1], SS[:, i:i + 1], 1.0 / D, EPS, ALU.mult, ALU.add),
                      reads=[SSt[i]], writes=[RSTDt[i]])
                sc.op("act", lambda e: e.activation(out=RSTD[:, i:i + 1], in_=RSTD[:, i:i + 1], func=AF.Sqrt),
                      reads=[], writes=[RSTDt[i]])
                sc.op("dve", lambda e: e.reciprocal(RSTD[:, i:i + 1], RSTD[:, i:i + 1]), reads=[], writes=[RSTDt[i]])

            def stC(i):
                if final:
                    for hf in range(2):
                        sc.op("dve", lambda e, hf=hf: e.scalar_tensor_tensor(
                            out=X[:, i, 512 * hf:512 * hf + 512], in0=X[:, i, 512 * hf:512 * hf + 512],
                            scalar=RSTD[:, i:i + 1], in1=Gn[:, 512 * hf:512 * hf + 512], op0=ALU.mult, op1=ALU.mult),
                            reads=[RSTDt[i]] + gb, writes=[Xb[i][hf]])
                    sc.dma("sp", out_v[i], X[:, i, :], OUTb[i % 4], reads=[Xb[i][0], Xb[i][1]])
                    return
                hn = PT[i % 2]
                sc.op("dve", lambda e: e.scalar_tensor_tensor(
                    out=hn, in0=X[:, i, :], scalar=RSTD[:, i:i + 1], in1=Gn, op0=ALU.mult, op1=ALU.mult),
                    reads=[Xb[i][0], Xb[i][1], RSTDt[i]] + gb, writes=[PTb[i % 2]])

            def stD(i):
                if final:
                    return
                hn = PT[i % 2]
                bank = i % 2
                pst = psb(bank, BF16).rearrange("p (c t) -> p c t", c=8)
                sc.pe_group([lambda e, c=c: e.transpose(pst[:, c, :], hn[:, 128 * c:128 * c + 128], IDENT)
                             for c in range(8)],
                            reads=[PTb[i % 2], IDENTb], writes=[PSb[bank]])

            def stE(i):
                if final:
                    return
                bank = i % 2
                pst = psb(bank, BF16).rearrange("p (c t) -> p c t", c=8)
                sc.op("act", lambda e: e.activation(out=HT[:, :, 128 * i:128 * i + 128], in_=pst, func=AF.Copy),
                      reads=[PSb[bank]], writes=[HTb[i]])
            stages = [stA, stB, stC, stD, stE]

            def after_tile(i):
                for k_, st in enumerate(stages):
                    if 0 <= i - k_ < NT:
                        st(i - k_)

            def flush():
                for i in range(NT, NT + len(stages) - 1):
                    after_tile(i)
            return begin, after_tile, flush

        def dump(ap2d, reads, ncols):
            sc.dma("pool", dbg_d[:, 0:ncols], ap2d, DBGb, reads=reads)

        def finish():
            sc.wait_all("sp", OUTb + [DBGb])

        def load_wqk(l, hh):
            sc.dma("pool", WQK, wqk_d[l * 16 + hh], WQKb)

        def load_wv(l, pp):
            sc.dma("pool", WV, wv_d[l * 8 + pp], WVb)

        head_order = []
        for i in range(8):
            head_order.append((0, i))
            head_order.append((1, i))

        def split_group(steps, fns, reads, writes, per_step, after=None, req=None):
            n = len(fns)
            pos = list(range(0, n, per_step))
            holder = {}
            for si, p0 in enumerate(pos):
                part = fns[p0:p0 + per_step]
                first = si == 0
                last = si == len(pos) - 1

                def st(part=part, first=first, last=last):
                    if first:
                        holder["h"] = sc.pe_open(reads=reads, writes=writes)
                    for f in (part[:-1] if last else part):
                        sc.pe_part(f)
                    if last:
                        sc.pe_close(holder["h"], part[-1])
                        if after is not None:
                            after()
                if req is not None:
                    st.req = req
                steps.append(st)

        def fox_layer_chunks(l, fbank=7):
            ch = []
            p7 = psb(fbank)
            v3 = lambda a: a[:, 0:128].rearrange("p (i h) -> p i h", h=8)
            f0_ = lambda: sc.dma("pool", WF[:, 0:64], wf_d[l], WFb)
            f0_.req = 0
            ch.append(f0_)
            for i0 in range(0, NT, 2):
                def c_f(i0=i0):
                    for i in (i0, i0 + 1):
                        sc.pe_group([lambda e, i=i, kc=kc: e.matmul(p7[:, 8 * i:8 * i + 8], lhsT=HT[:, kc, 128 * i:128 * i + 128],
                                                                    rhs=WF[:, 8 * kc:8 * kc + 8], start=(kc == 0), stop=(kc == 7))
                                     for kc in range(8)],
                                    reads=[HTb[i], WFb], writes=[PSb[fbank]])
                c_f.req = i0 + 1
                ch.append(c_f)

            def c_lf():
                bfb = BFT[:, 8 * l:8 * l + 8].unsqueeze(1).to_broadcast([128, NT, 8])
                sc.op("dve", lambda e: e.tensor_tensor(out=v3(FL), in0=v3(p7), in1=bfb, op=ALU.add),
                      reads=[PSb[fbank], BFTb], writes=[FLb])
                sc.op("act", lambda e: e.activation(out=FL[:, 0:128], in_=FL[:, 0:128], func=AF.Exp, scale=-1.0),
                      reads=[], writes=[FLb])
                sc.op("act", lambda e: e.activation(out=FL[:, 0:128], in_=FL[:, 0:128], func=AF.Ln, bias=1.0),
                      reads=[], writes=[FLb])
            ch.append(c_lf)
            ch.append(lambda: None)
            ch.append(lambda: None)

            def c_px0():
                sc.op("dve", lambda e: e.tensor_scalar_mul(LOGF[:, 0:128], FL[:, 0:128], -1.0),
                      reads=[FLb], writes=[LOGFb])
            ch.append(c_px0)
            for i in range(1, NT):
                ch.append(lambda i=i: sc.op("dve", lambda e: e.tensor_tensor(
                    out=PSX[:, 8 * i:8 * i + 8], in0=PSX[:, 8 * i - 8:8 * i], in1=LOGF[:, 8 * i - 8:8 * i], op=ALU.add),
                    reads=[LOGFb], writes=[PSXb]))
            fns = []
            for i in range(NT):
                fns.append(lambda e, i=i: e.matmul(p7[:, 8 * i:8 * i + 8], lhsT=Um, rhs=LOGF[:, 8 * i:8 * i + 8],
                                                   start=True, stop=False))
                fns.append(lambda e, i=i: e.matmul(p7[:, 8 * i:8 * i + 8], lhsT=ONES[:, 0:128], rhs=PSX[:, 8 * i:8 * i + 8],
                                                   start=False, stop=True))
            split_group(ch, fns, [CSTb, ONESb, LOGFb, PSXb], [PSb[fbank]], 8)
            ch.append(lambda: None)
            ch.append(lambda: sc.op("dve", lambda e: e.tensor_scalar_mul(NEGC[:, 0:128], p7[:, 0:128], -1.0),
                                    reads=[PSb[fbank]], writes=[NEGCb]))
            cps3 = CPS[0:8, 0:1536].rearrange("p (k t) -> p k t", k=3)
            for r in range(4):
                fns = []
                for ti in range(4):
                    i = 4 * r + ti
                    fns.append(lambda e, i=i, ti=ti: e.matmul(p7[0:8, 128 * ti:128 * ti + 128], lhsT=LOGF[:, 8 * i:8 * i + 8],
                                                              rhs=Um, start=True, stop=False))
                    fns.append(lambda e, i=i, ti=ti: e.matmul(p7[0:8, 128 * ti:128 * ti + 128], lhsT=PSX[:, 8 * i:8 * i + 8],
                                                              rhs=ONES[:, 0:128], start=False, stop=True))
                split_group(ch, fns, [CSTb, ONESb, LOGFb, PSXb], [PSb[fbank]], 2)
                ch.append(lambda: None)

                def c_r2(r=r):
                    sc.op("dve", lambda e: e.tensor_copy(out=cps3[:, 0, :], in_=p7[0:8, :]), reads=[PSb[fbank]], writes=[CPSb])
                    sc.op("dve", lambda e: e.tensor_tensor(out=CR[0:8, :], in0=p7[0:8, :], in1=cps3[:, 0, :], op=ALU.subtract),
                          reads=[PSb[fbank]], writes=[CRb, CPSb])
                ch.append(c_r2)

                def c_r3(r=r):
                    sc.op("dve", lambda e: e.tensor_copy(out=cps3[:, 1, :], in_=CR[0:8, :]), reads=[], writes=[CPSb, CRb])
                    sc.op("dve", lambda e: e.tensor_tensor(out=CR[0:8, :], in0=CR[0:8, :], in1=cps3[:, 1, :], op=ALU.subtract),
                          reads=[], writes=[CRb, CPSb])
                ch.append(c_r3)

                def c_r4(r=r):
                    sc.op("dve", lambda e: e.tensor_copy(out=cps3[:, 2, :], in_=CR[0:8, :]), reads=[], writes=[CPSb, CRb])
                    sc.dma("sp", cpd_d[:, :, 512 * r:512 * r + 512], cps3, CPDb, reads=[CPSb])
                ch.append(c_r4)
            return ch

        def prep_chunks(l, idx):
            typ, hi = head_order[idx]
            hh = hi + 8 * typ
            par = typ
            ch = []

            def c_aug():
                sc.dma("pool", KT[par][64:72, :], kaug_d[typ], KTa[par])
            c_aug.req = 0
            ch.append(c_aug)
            for tg in range(4):
                bank = 6 + tg % 2
                pp_ = psb(bank)
                fns = [lambda e, kc=kc, pp_=pp_, tg=tg: e.matmul(
                    pp_, lhsT=WQK[:, 128 * kc:128 * kc + 128],
                    rhs=HT[:, kc, 512 * tg:512 * tg + 512], start=(kc == 0), stop=(kc == 7))
                    for kc in range(8)]
                split_group(ch, fns, [WQKb] + HTb[4 * tg:4 * tg + 4], [PSb[bank]], 1, req=4 * tg + 3)
                sgi = STGI[0] % 4
                STGI[0] += 1

                def ev(pp_=pp_, tg=tg, bank=bank, sgi=sgi):
                    sc.op("dve", lambda e: e.tensor_scalar_mul(
                        QT[par][0:64, 512 * tg:512 * tg + 512], pp_[0:64, :], 0.125),
                        reads=[PSb[bank]], writes=[QTd[par]])
                    sc.op("dve", lambda e: e.tensor_copy(out=STG[sgi][64:128, :], in_=pp_[64:128, :]),
                          reads=[PSb[bank]], writes=[STGb[sgi]])
                    sc.dma("sp", KT[par][0:64, 512 * tg:512 * tg + 512], STG[sgi][64:128, :], KTd[par], reads=[STGb[sgi]])
                ev.req = 4 * tg + 3
                ch.append(("lag", ev))

            def c_wq():
                if idx + 1 < 16:
                    t2, h2 = head_order[idx + 1]
                    load_wqk(l, h2 + 8 * t2)
                elif l + 1 < n_layers:
                    load_wqk(l + 1, 0)
            ch.append(("lag", c_wq))
            if hi % 2 == 0:
                for q4 in range(4):
                    bank = 6 + q4 % 2
                    pv = psb(bank)
                    fns = []
                    for ti in range(4):
                        i = 4 * q4 + ti
                        for kc in range(8):
                            fns.append(lambda e, i=i, ti=ti, kc=kc, pv=pv: e.matmul(
                                pv[:, 128 * ti:128 * ti + 128], lhsT=HT[:, kc, 128 * i:128 * i + 128],
                                rhs=WV[:, 128 * kc:128 * kc + 128], start=(kc == 0), stop=(kc == 7)))
                    split_group(ch, fns, [WVb] + HTb[4 * q4:4 * q4 + 4], [PSb[bank]], 4)
                    pv3 = pv.rearrange("p (t c) -> p t c", c=128)

                    def ev(q4=q4, pv3=pv3, bank=bank):
                        sc.op("dve", lambda e: e.tensor_copy(out=VA[typ][:, 4 * q4:4 * q4 + 4, 0:64], in_=pv3[:, :, 0:64]),
                              reads=[PSb[bank]], writes=[Vb[typ]])
                        sc.op("dve", lambda e: e.tensor_copy(out=VB[typ][:, 4 * q4:4 * q4 + 4, 64:128], in_=pv3[:, :, 64:128]),
                              reads=[PSb[bank]], writes=[Vb[typ]])
                    ch.append(("lag", ev))

                def c_wv():
                    nxt = None
                    for j2 in range(idx + 1, 16):
                        t2, h2 = head_order[j2]
                        if h2 % 2 == 0:
                            nxt = (l, h2 // 2 + 4 * t2)
                            break
                    if nxt is None and l + 1 < n_layers:
                        nxt = (l + 1, 0)
                    if nxt is not None:
                        load_wv(*nxt)
                ch.append(("lag", c_wv))
            ch.append(lambda: None)
            ch.append(lambda: None)
            ch.append(lambda: None)
            if typ == 1:
                ch.append(lambda: sc.dma("sp", QT[1][64:67, :], cpd_d[hi], QTa[1], reads=[CPDb]))
            else:
                p7 = psb(7)
                g0, g1, e1 = GT[0][:, 0:128], GT[1][:, 0:128], GT[2][:, 0:128]
                v3 = lambda a: a.rearrange("p (i n) -> p i n", n=8)
                mxb = lambda: MX[:, 0:16].unsqueeze(2).to_broadcast([128, NT, 8])

                def c_g1():
                    sc.op("dve", lambda e: e.tensor_reduce(out=KM[0:64, 0:8], in_=KT[0][0:64, :].rearrange("p (n s) -> p n s", s=256),
                                                           axis=AX.X, op=ALU.add),
                          reads=[KTd[0]], writes=[KMb])
                ch.append(c_g1)
                ch.append(lambda: None)
                ch.append(lambda: None)

                def c_g1b():
                    sc.op("dve", lambda e: e.tensor_copy(out=KMH[0:64, 0:8], in_=KM[0:64, 0:8]), reads=[], writes=[KMHb, KMb])
                    sc.op("dve", lambda e: e.tensor_tensor(out=KMR[0:64, 0:8], in0=KM[0:64, 0:8], in1=KMH[0:64, 0:8], op=ALU.subtract),
                          reads=[], writes=[KMHb, KMb])
                    sc.op("dve", lambda e: e.tensor_copy(out=KML[0:64, 0:8], in_=KMR[0:64, 0:8]), reads=[], writes=[KMHb, KMb])
                ch.append(c_g1b)
                for _ in range(4):
                    ch.append(lambda: None)
                fns = []
                for i in range(NT):
                    fns.append(lambda e, i=i: e.matmul(p7[:, 8 * i:8 * i + 8], lhsT=QT[0][0:64, 128 * i:128 * i + 128],
                                                       rhs=KMH[0:64, 0:8], start=True, stop=False))
                    fns.append(lambda e, i=i: e.matmul(p7[:, 8 * i:8 * i + 8], lhsT=QT[0][0:64, 128 * i:128 * i + 128],
                                                       rhs=KML[0:64, 0:8], start=False, stop=True))
                split_group(ch, fns, [QTd[0], KMHb], [PSb[7]], 8)
                for _ in range(4):
                    ch.append(lambda: None)

                def c_g3a():
                    sc.op("dve", lambda e: e.tensor_tensor(out=g0, in0=p7[:, 0:128], in1=PM, op=ALU.add),
                          reads=[PSb[7], CSTb], writes=[GTb[0]])
                ch.append(c_g3a)
                src = g0
                srcb = GTb[0]
                for rnd in range(2):
                    dst = g1 if rnd == 0 else e1
                    dstb = GTb[1] if rnd == 0 else GTb[2]

                    def c_rnd(src=src, srcb=srcb, dst=dst, dstb=dstb):
                        sc.op("dve", lambda e: e.tensor_reduce(out=MX[:, 0:16], in_=v3(src), axis=AX.X, op=ALU.max),
                              reads=[srcb], writes=[MXb])
                        sc.op("dve", lambda e: e.tensor_tensor(out=v3(GT[3][:, 0:128]), in0=v3(src), in1=mxb(), op=ALU.is_ge),
                              reads=[srcb, MXb], writes=[GTb[3]])
                        sc.op("dve", lambda e: e.scalar_tensor_tensor(out=dst, in0=GT[3][:, 0:128], scalar=-BIG, in1=src,
                                                                      op0=ALU.mult, op1=ALU.add),
                              reads=[srcb, GTb[3]], writes=[dstb])
                    ch.append(c_rnd)
                    src = dst
                    srcb = dstb

                def c_g3b(src=src, srcb=srcb):
                    sc.op("dve", lambda e: e.tensor_reduce(out=MX[:, 0:16], in_=v3(src), axis=AX.X, op=ALU.max),
                          reads=[srcb], writes=[MXb])
                    sc.op("dve", lambda e: e.tensor_tensor(out=v3(GT[3][:, 0:128]), in0=v3(g0), in1=mxb(), op=ALU.is_ge),
                          reads=[GTb[0], MXb], writes=[GTb[3]])
                    sc.op("dve", lambda e: e.tensor_scalar(GT[3][:, 0:128], GT[3][:, 0:128], -1.0, BIG, ALU.add, ALU.mult),
                          reads=[], writes=[GTb[3]])
                    sc.op("dve", lambda e: e.tensor_tensor(out=MVB[:, 0:128], in0=GT[3][:, 0:128], in1=NOTOWN, op=ALU.mult),
                          reads=[GTb[3], CSTb], writes=[MVBb])
                ch.append(c_g3b)
                for _ in range(4):
                    ch.append(lambda: None)
                mt = CPS[0:8, 0:1024]
                for half in range(2):
                    def c_g4(half=half):
                        pst = psb(7, BF16)
                        sc.pe_group([lambda e, t=t: e.transpose(
                            pst[0:8, 128 * t:128 * t + 128], MVB[:, 8 * (8 * half + t):8 * (8 * half + t) + 8], IDENT)
                            for t in range(8)],
                            reads=[MVBb, IDENTb], writes=[PSb[7]])
                    ch.append(c_g4)
                    for _ in range(3):
                        ch.append(lambda: None)

                    def c_g5(half=half):
                        pst = psb(7, BF16)
                        sc.op("dve", lambda e: e.tensor_copy(out=mt, in_=pst[0:8, 0:1024]), reads=[PSb[7]], writes=[CPSb])
                        sc.dma("sp", QT[0][64:72, 1024 * half:1024 * half + 1024], mt, QTa[0], reads=[CPSb])
                    ch.append(c_g5)
            if typ == 0:
                ch.append(lambda: sc.dma("sp", ESTRIP, ts_d[hi], STRIPb, reads=[TSb[hi]]))
            out = []
            lagq = []
            for c in ch:
                for q in lagq:
                    q[0] += 1
                while lagq and lagq[0][0] >= 2:
                    f = lagq.pop(0)[1]
                    out.append(f)
                if isinstance(c, tuple):
                    lagq.append([0, c[1]])
                else:
                    out.append(c)
            for q in lagq:
                out.append(q[1])
            return out

        deferred = []

        def tick_deferred(flush=False):
            for d_ in deferred:
                d_[0] -= 1
            i_ = 0
            while i_ < len(deferred):
                if flush or deferred[i_][0] <= 0:
                    deferred.pop(i_)[1]()
                else:
                    i_ += 1

        def make_head(l, idx):
            typ, hi = head_order[idx]
            par = typ
            Bl = hi % 2
            chunk = hi // 2 + 4 * typ
            mv = 128 if Bl else 65
            tiles = [(gp, j) for gp in range(2) for j in range(8 * (gp + 1))]
            nT = len(tiles)

            def geom(n):
                gp, j = tiles[n]
                base = 1024 * gp
                c0 = max(base, 128 * j)
                c1 = base + 1024
                if c0 < base + 512:
                    pieces = [(c0, base + 512), (base + 512, c1)]
                else:
                    pieces = [(c0, c1)]
                return gp, j, base, c0, c1, pieces

            def sset(n):
                st_ = n % 2
                return PSt[:, 1024 * st_:1024 * st_ + 1024], [PSb[2 * st_], PSb[2 * st_ + 1]]

            def qk(n):
                gp, j, base, c0, c1, pieces = geom(n)
                SSv, sb = sset(n)
                diag = (typ == 1 and j >= 8 * gp)
                fns = []
                for pi, (cs, ce) in enumerate(pieces):
                    fns.append(lambda e, cs=cs, ce=ce, pi=pi: e.matmul(
                        SSv[:, cs - base:ce - base], lhsT=KT[par][0:72, 128 * j:128 * j + 128], rhs=QT[par][0:72, cs:ce],
                        start=True, stop=not (diag and pi == 0)))
                    if diag and pi == 0:
                        fns.append(lambda e: e.matmul(SSv[:, c0 - base:c0 - base + 128], lhsT=IDENT, rhs=CM[:, 0:128],
                                                      start=False, stop=True))
                rd = [KTd[par], KTa[par], QTd[par], QTa[par]]
                if diag:
                    rd += [IDENTb, CMb]
                sc.pe_group(fns, reads=rd, writes=sb)

            def soft(n):
                gp, j, base, c0, c1, pieces = geom(n)
                SSv, sb = sset(n)
                pk = n % 3
                lo = c0 - base
                if typ == 0:
                    off = c0 - 128 * j
                    w = c1 - c0
                    sc.op("act", lambda e: e.activation(out=PT[pk][:, lo:1024], in_=SSv[:, lo:1024], func=AF.Exp),
                          reads=sb, writes=[PTb[pk]])
                    sc.op("dve", lambda e: e.tensor_tensor(out=PT[pk][:, lo:1024], in0=PT[pk][:, lo:1024],
                                                           in1=ESTRIP[:, off:off + w], op=ALU.mult),
                          reads=[STRIPb], writes=[PTb[pk]])
                else:
                    sc.op("act", lambda e: e.activation(out=PT[pk][:, lo:1024], in_=SSv[:, lo:1024], func=AF.Exp,
                                                        bias=NEGC[:, 8 * j + hi:8 * j + hi + 1]),
                          reads=sb + [NEGCb], writes=[PTb[pk]])

            def pv(n):
                gp, j, base, c0, c1, pieces = geom(n)
                pk = n % 3
                lh = VB[typ][:, j, :] if Bl else VA[typ][:, j, 0:65]
                for (cs, ce) in pieces:
                    g = cs // 512
                    ob = 4 + g % 2
                    po = psb(ob)
                    sc.pe_group([lambda e, cs=cs, ce=ce, g=g, po=po: e.matmul(
                        po[0:mv, cs - 512 * g:ce - 512 * g], lhsT=lh, rhs=PT[pk][:, cs - base:ce - base],
                        start=(j == 0), stop=(j == 4 * g + 3))],
                        reads=[Vb[typ], PTb[pk]], writes=[PSb[ob]])
                    if j == 4 * g + 3:
                        epilogue(g, ob, po)

            def epilogue(g, ob, po):
                rows = slice(0, 128) if Bl else slice(0, 65)
                orow = slice(64, 128) if Bl else slice(0, 64)
                srow = 0 if Bl else 64
                kb = (4 * idx + g) % 2
                osb, bc = OSB2[kb], BC2[kb]
                def st1():
                    sc.op("dve", lambda e: e.tensor_copy(out=osb[rows, :], in_=po[rows, :]),
                          reads=[PSb[ob]], writes=[OSB2b[kb]])

                def st2():
                    sc.op("act", lambda e: e.activation(out=osb[srow:srow + 1, :], in_=osb[srow:srow + 1, :], func=AF.Ln),
                          reads=[], writes=[OSB2b[kb]])
                    sc.op("act", lambda e: e.activation(out=osb[srow:srow + 1, :], in_=osb[srow:srow + 1, :], func=AF.Exp, scale=-1.0),
                          reads=[], writes=[OSB2b[kb]])
                    sc.dma("sp", sum_d[kb:kb + 1, :], osb[srow:srow + 1, :], SUMb[kb], reads=[OSB2b[kb]])
                    sc.dma("sp", bc[orow, :], sum_d[kb:kb + 1, :].to_broadcast([64, 512]), BC2b[kb], reads=[SUMb[kb]])

                def st3():
                    sc.op("dve", lambda e: e.tensor_tensor(
                        out=YT[orow, chunk, 512 * g:512 * g + 512], in0=osb[orow, :], in1=bc[orow, :], op=ALU.mult),
                        reads=[OSB2b[kb], BC2b[kb]], writes=[YTb[chunk][g]])
                deferred.append([1, st1])
                deferred.append([2, st2])
                deferred.append([6, st3])

            return nT, qk, soft, pv

        def outproj(l):
            wo3 = WOv.rearrange("p (c n) -> p c n", c=8)
            k = 0
            nb, nafter, nflush = make_norm(2 * l + 1, False)
            nb()
            for i in range(NT):
                if i >= 1:
                    nafter(i - 1)
                for hf in range(2):
                    bank = 5 + k % 2
                    k += 1
                    ps = psb(bank)
                    sc.pe_group([lambda e, c=c, i=i, hf=hf, ps=ps: e.matmul(
                        ps, lhsT=YT[:, c, 128 * i:128 * i + 128], rhs=wo3[:, c, 512 * hf:512 * hf + 512],
                        start=(c == 0), stop=(c == 7)) for c in range(8)],
                        reads=[WOb] + WO_OVER + [YTb[c][i // 4] for c in range(8)], writes=[PSb[bank]])
                    sc.op("dve", lambda e, i=i, hf=hf, ps=ps: e.tensor_tensor(
                        out=X[:, i, 512 * hf:512 * hf + 512], in0=X[:, i, 512 * hf:512 * hf + 512], in1=ps, op=ALU.add),
                        reads=[PSb[bank]], writes=[Xb[i][hf]])
            nafter(NT - 1)
            nflush()

        def load_gu(l, fc):
            sc.dma("pool", WG[fc % 2], wg_d[l * NFC + fc], WGb[fc % 2])
            sc.dma("pool", WU[fc % 2], wu_d[l * NFC + fc], WUb[fc % 2])

        def ffn(l):
            supers = [(0, 6), (6, 12), (12, 17), (17, 22)]
            kk = 0
            for (a, b) in supers:
                for fc in range(a, b):
                    s_ = fc - a
                    sc.dma("pool", WDv[s_], wd_d[l * NFC + fc], WDb[s_], also_writes=WD_OVER[s_])
                for fc in range(a, b):
                    fci = fc - a
                    if fc + 1 < NFC:
                        load_gu(l, fc + 1)
                    sl = fc % 2
                    for tg in range(4):
                        bg = kk % 2
                        bu = 2 + kk % 2
                        pk = kk % 3
                        kk += 1
                        pg, pu = psb(bg), psb(bu)
                        sc.pe_group([lambda e, kc=kc, tg=tg, pg=pg, sl=sl: e.matmul(
                            pg, lhsT=WG[sl][:, 128 * kc:128 * kc + 128], rhs=HT[:, kc, 512 * tg:512 * tg + 512],
                            start=(kc == 0), stop=(kc == 7)) for kc in range(8)],
                            reads=[WGb[sl]] + HTb[4 * tg:4 * tg + 4], writes=[PSb[bg]])
                        sc.pe_group([lambda e, kc=kc, tg=tg, pu=pu, sl=sl: e.matmul(
                            pu, lhsT=WU[sl][:, 128 * kc:128 * kc + 128], rhs=HT[:, kc, 512 * tg:512 * tg + 512],
                            start=(kc == 0), stop=(kc == 7)) for kc in range(8)],
                            reads=[WUb[sl]] + HTb[4 * tg:4 * tg + 4], writes=[PSb[bu]])
                        sc.op("act", lambda e, pg=pg, pk=pk: e.activation(out=PT[pk][:, 0:512], in_=pg, func=AF.Silu),
                              reads=[PSb[bg]], writes=[PTb[pk]])
                        sc.op("dve", lambda e, pu=pu, pk=pk, fci=fci, tg=tg: e.tensor_tensor(
                            out=YT[:, fci, 512 * tg:512 * tg + 512], in0=PT[pk][:, 0:512], in1=pu, op=ALU.mult),
                            reads=[PTb[pk], PSb[bu]], writes=[YTb[fci][tg]])
                nch = b - a
                k = 0
                lastsc = (b == NFC)
                if lastsc:
                    ss_begin()
                for i in range(NT):
                    if lastsc and i >= 1:
                        ss_tile(i - 1)
                    for hf in range(2):
                        bank = 5 + k % 2
                        k += 1
                        ps = psb(bank)
                        rd = []
                        for s_ in range(nch):
                            rd += [WDb[s_]] + WD_OVER[s_] + [YTb[s_][i // 4]]
                        sc.pe_group([lambda e, s_=s_, i=i, hf=hf, ps=ps, nch=nch: e.matmul(
                            ps, lhsT=YT[:, s_, 128 * i:128 * i + 128], rhs=WDv[s_][:, 512 * hf:512 * hf + 512],
                            start=(s_ == 0), stop=(s_ == nch - 1)) for s_ in range(nch)],
                            reads=rd, writes=[PSb[bank]])
                        sc.op("dve", lambda e, i=i, hf=hf, ps=ps: e.tensor_tensor(
                            out=X[:, i, 512 * hf:512 * hf + 512], in0=X[:, i, 512 * hf:512 * hf + 512], in1=ps, op=ALU.add),
                            reads=[PSb[bank]], writes=[Xb[i][hf]])
                if lastsc:
                    ss_tile(NT - 1)

        def program():
            load_wqk(0, 0)
            load_wv(0, 0)
            for l in range(n_layers):
                fch = fox_layer_chunks(l, fbank=4)
                p0 = prep_chunks(l, 0)
                ptr = [0, 0]

                def hook(i, p0=p0, fch=fch, ptr=ptr):
                    n_ = 0
                    while ptr[0] < len(p0) and getattr(p0[ptr[0]], "req", 99) <= i and n_ < 12:
                        p0[ptr[0]]()
                        ptr[0] += 1
                        n_ += 1
                    n_ = 0
                    while ptr[1] < len(fch) and getattr(fch[ptr[1]], "req", 99) <= i and n_ < 4:
                        fch[ptr[1]]()
                        ptr[1] += 1
                        n_ += 1
                emit_norm(2 * l, ss_done=(l > 0), hook=(hook if l > 0 else None))
                if l == 0:
                    t5_prologue()
                if stop_after == "n1":
                    dump(HTt[:], HTb, 16384)
                    return
                i0, j0 = ptr
                while i0 < len(p0) or j0 < len(fch):
                    for _ in range(2):
                        if i0 < len(p0):
                            p0[i0]()
                            i0 += 1
                    if j0 < len(fch):
                        fch[j0]()
                        j0 += 1
                fch = []
                heads = []
                chlists = []
                for idx in range(16):
                    heads.append(make_head(l, idx))
                for idx in range(16):
                    chs = []
                    if idx == 0:
                        chs += fch
                    if idx + 1 < 16:
                        chs += prep_chunks(l, idx + 1)
                    else:
                        def c_wo(l=l):
                            for c_ in range(8):
                                sc.dma("pool", WOv[:, 1024 * c_:1024 * c_ + 1024], wo_d[l, :, 1024 * c_:1024 * c_ + 1024],
                                       WOb, also_writes=WO_OVER)
                        chs.append(c_wo)
                    chlists.append(chs)
                nstop = 16
                if stop_after is not None and stop_after.startswith("m") and stop_after[1:].isdigit():
                    nstop = int(stop_after[1:]) + 1
                heads[0][1](0)
                for idx in range(nstop):
                    nT, qk, soft, pv = heads[idx]
                    chunks = chlists[idx]
                    nch = len(chunks)
                    done = 0
                    for n in range(nT):
                        soft(n)
                        tick_deferred()
                        want = (nch * (n + 1) + nT - 7) // (nT - 6)
                        if n + 1 < nT:
                            qk(n + 1)
                        else:
                            want = nch
                        while done < min(want, nch):
                            chunks[done]()
                            done += 1
                        if n + 1 == nT and idx + 1 < nstop:
                            heads[idx + 1][1](0)
                        pv(n)
                    assert done == nch
                if nstop < 16:
                    tick_deferred(flush=True)
                    dump(YTt[:], [b for bb in YTb for b in bb], 16384)
                    return
                tick_deferred(flush=True)
                if stop_after == "att":
                    dump(YTt[:], [b for bb in YTb for b in bb], 16384)
                    return
                outproj(l)
                if stop_after == "xatt":
                    sc.dma("sp", dbg_d, Xt[:], DBGb, reads=[b for bb in Xb for b in bb])
                    return
                load_gu(l, 0)
                ffn(l)
                if stop_after == "l0":
                    sc.dma("sp", dbg_d, Xt[:], DBGb, reads=[b for bb in Xb for b in bb])
                    return
            emit_norm(8, final=True, ss_done=True)

        program()
        finish()

        @block.tensor
        def _(e):
            sc.run("pe", e)

        @block.scalar
        def _(e):
            sc.run("act", e)

        @block.vector
        def _(e):
            sc.run("dve", e)

        @block.gpsimd
        def _(e):
            sc.run("pool", e)

        @block.sync
        def _(e):
            sc.run("sp", e)

    return nc


def _t5_bucket_np(dist):
    max_exact = 16
    d = np.maximum(dist, 1).astype(np.float32)
    large = max_exact + (np.log(d / max_exact) / math.log(1024 / max_exact) * (32 - max_exact)).astype(np.int32)
    large = np.minimum(large, 31)
    return np.where(dist < max_exact, dist, large)


def _constants():
    cst = np.zeros((128, 512), np.float32)
    cst[:, 0:128] = np.eye(128, dtype=np.float32)[::-1]
    s_ = np.arange(128)
    cst[:, 128:256] = (s_[:, None] <= s_[None, :]).astype(np.float32)
    pm = np.zeros((16, 8), np.float32)
    no = np.ones((16, 8), np.float32)
    for i in range(16):
        qb = i // 2
        pm[i, qb:] = -BIG
        no[i, qb] = 0.0
    cst[:, 256:384] = pm.reshape(1, 128)
    cst[:, 384:512] = no.reshape(1, 128)
    ident = np.eye(128, dtype=np.float32)
    kaug = np.zeros((2, 8, S), np.float32)
    for n in range(8):
        kaug[0, n, 256 * n:256 * n + 256] = 1.0
    kaug[1, 0:3, :] = 1.0
    ohb = np.zeros((33, EXTW), np.float32)
    u = np.arange(EXTW)
    dd = u - 127
    bk = _t5_bucket_np(np.maximum(dd, 0))
    for uu in range(EXTW):
        if dd[uu] >= 0:
            ohb[bk[uu], uu] = 1.0
        else:
            ohb[32, uu] = -BIG
    return cst, ident, kaug, ohb


def _prep_weights(w_in, b_f, w_o, g_attn, w_gu, w_down, g_ffn, rel_bias, g_final):
    f32 = np.float32
    w_in = np.asarray(w_in, f32)
    wk = w_in.reshape(L, 8, 128, 3080)
    wqk = np.empty((L, 16, 128, 8, 128), f32)
    wv = np.empty((L, 8, 128, 8, 128), f32)
    for typ in range(2):
        qo = 0 if typ == 0 else 1536
        ko = 512 if typ == 0 else 2048
        vo = 1024 if typ == 0 else 2560
        for hi in range(8):
            hh = hi + 8 * typ
            wqk[:, hh, :, :, 0:64] = wk[:, :, :, qo + 64 * hi:qo + 64 * hi + 64].transpose(0, 2, 1, 3)
            wqk[:, hh, :, :, 64:128] = wk[:, :, :, ko + 64 * hi:ko + 64 * hi + 64].transpose(0, 2, 1, 3)
        for pp in range(4):
            wv[:, pp + 4 * typ] = wk[:, :, :, vo + 128 * pp:vo + 128 * pp + 128].transpose(0, 2, 1, 3)
    wf = wk[:, :, :, 3072:3080].transpose(0, 2, 1, 3)
    wo = np.asarray(w_o, f32).reshape(L, 8, 128, 1024).transpose(0, 2, 1, 3)
    wgu = np.asarray(w_gu, f32).reshape(L, 8, 128, 2 * DFF)
    wg = wgu[:, :, :, 0:DFF].reshape(L, 8, 128, NFC, 128).transpose(0, 3, 2, 1, 4)
    wu = wgu[:, :, :, DFF:].reshape(L, 8, 128, NFC, 128).transpose(0, 3, 2, 1, 4)
    wd = np.asarray(w_down, f32).reshape(L * NFC, 128, 1024)
    gall = np.empty((9, D), f32)
    for l in range(L):
        gall[2 * l] = g_attn[l]
        gall[2 * l + 1] = g_ffn[l]
    gall[8] = g_final
    cst, ident, kaug, ohb = _constants()
    return {
        "wqk": np.ascontiguousarray(wqk.reshape(L * 16, 128, 1024)),
        "wv": np.ascontiguousarray(wv.reshape(L * 8, 128, 1024)),
        "wf": np.ascontiguousarray(wf.reshape(L, 128, 64)),
        "wo": np.ascontiguousarray(wo.reshape(L, 128, 8192)),
        "wg": np.ascontiguousarray(wg.reshape(L * NFC, 128, 1024)),
        "wu": np.ascontiguousarray(wu.reshape(L * NFC, 128, 1024)),
        "wd": np.ascontiguousarray(wd),
        "gall": gall,
        "bfl": np.ascontiguousarray(np.asarray(b_f, f32).reshape(1, 32)),
        "rb": np.ascontiguousarray(np.asarray(rel_bias, f32)),
        "cst": cst, "ident": ident, "kaug": kaug, "ohb": ohb,
    }


def kernel(x, w_in, b_f, w_o, g_attn, w_gu, w_down, g_ffn, rel_bias, g_final):
    x = np.asarray(x, np.float32)
    shared = _prep_weights(w_in, b_f, w_o, g_attn, w_gu, w_down, g_ffn, rel_bias, g_final)
    nc = build_program()
    in_maps = []
    for c in range(N_CORES):
        m = dict(shared)
        m["x"] = np.ascontiguousarray(x[c])
        in_maps.append(m)
    res = run_bass_kernel_spmd(nc, in_maps, core_ids=list(range(N_CORES)))
    return np.stack([np.asarray(r["out"], np.float32) for r in res.results], axis=0)
```

```python
import contextlib
import math
import numpy as np
import concourse.bass as bass
import concourse.mybir as mybir
from concourse.bass_utils import run_bass_kernel_spmd

F32 = mybir.dt.float32
BF16 = mybir.dt.bfloat16
U8 = mybir.dt.uint8
AF = mybir.ActivationFunctionType
ALU = mybir.AluOpType
AX = mybir.AxisListType

S = 2048
D = 1024
NT = 16
L = 4
DFF = 2816
NFC = 22
BIG = 30000.0
EPS = 1e-6
EXTW = 2176
N_CORES = 8


class Buf:
    __slots__ = ("name", "w", "r", "dsem", "dcnt")

    def __init__(self, name):
        self.name = name
        self.w = []
        self.r = []
        self.dsem = None
        self.dcnt = 0


class Sched:
    ENG = ("pe", "act", "dve", "pool", "sp")

    def __init__(self, sems):
        self.ops = {e: [] for e in self.ENG}
        self.cnt = {e: 0 for e in self.ENG}
        self.waited = {e: {} for e in self.ENG}
        self.esem = {e: sems[i] for i, e in enumerate(self.ENG)}
        self.free_sems = list(sems[len(self.ENG):])
        self.semh = {}
        for e in self.ENG:
            self.semh[e] = self.esem[e]

    def _need(self, eng, t):
        key, val = t
        if key == "pe" and eng == "pe":
            return
        w = self.waited[eng]
        if w.get(key, 0) >= val:
            return
        w[key] = val
        self.ops[eng].append(("wait", key, val))

    @staticmethod
    def _addr(b, tk):
        for i, (k, v) in enumerate(b.r):
            if k == tk[0]:
                if v < tk[1]:
                    b.r[i] = tk
                return
        b.r.append(tk)

    def _deps(self, eng, reads, writes, extra):
        for t in extra:
            self._need(eng, t)
        for b in reads:
            for t in b.w:
                self._need(eng, t)
        for b in writes:
            for t in b.w:
                self._need(eng, t)
            for t in b.r:
                self._need(eng, t)

    def _finish(self, tk, reads, writes):
        for b in writes:
            b.w = [tk]
            b.r = []
        for b in reads:
            if b not in writes:
                self._addr(b, tk)

    def op(self, eng, fn, reads=(), writes=(), extra=()):
        self._deps(eng, reads, writes, extra)
        self.cnt[eng] += 1
        tk = (eng, self.cnt[eng])
        self.ops[eng].append(("op", fn, True))
        self._finish(tk, reads, writes)
        return tk

    def pe_group(self, fns, reads=(), writes=()):
        self._deps("pe", reads, writes, ())
        self.cnt["pe"] += 1
        tk = ("pe", self.cnt["pe"])
        for i, fn in enumerate(fns):
            self.ops["pe"].append(("op", fn, i == len(fns) - 1))
        self._finish(tk, reads, writes)
        return tk

    def pe_open(self, reads=(), writes=()):
        self._deps("pe", reads, writes, ())
        return (list(reads), list(writes))

    def pe_part(self, fn):
        self.ops["pe"].append(("op", fn, False))

    def pe_close(self, handle, fn):
        reads, writes = handle
        self.cnt["pe"] += 1
        tk = ("pe", self.cnt["pe"])
        self.ops["pe"].append(("op", fn, True))
        self._finish(tk, reads, writes)
        return tk

    def dma(self, q, out_ap, in_ap, dst, reads=(), also_writes=(), **kw):
        writes = [dst] + list(also_writes)
        self._deps(q, reads, writes, ())
        if dst.dsem is None:
            dst.dsem = self.free_sems.pop()
            self.semh[("d", id(dst))] = dst.dsem
        dst.dcnt += 16
        tk = (("d", id(dst)), dst.dcnt)
        self.ops[q].append(("dma", out_ap, in_ap, ("d", id(dst)), kw))
        self._finish(tk, reads, writes)
        return tk

    def wait_all(self, eng, bufs):
        for b in bufs:
            for t in b.w:
                self._need(eng, t)

    def run(self, eng, e):
        semh = self.semh
        for o in self.ops[eng]:
            if o[0] == "wait":
                e.wait_ge(semh[o[1]], o[2])
            elif o[0] == "op":
                ins = o[1](e)
                if o[2]:
                    ins.then_inc(semh[eng], 1)
            else:
                e.dma_start(out=o[1], in_=o[2], **o[4]).then_inc(semh[o[3]], 16)


def build_program(n_layers=L, stop_after=None):
    nc = bass.Bass("TRN2", target_bir_lowering=False)
    dbg = stop_after is not None

    def din(name, shape):
        return nc.dram_tensor(name, list(shape), F32, kind="ExternalInput").ap()

    x_d = din("x", [S, D])
    wqk_d = din("wqk", [L * 16, 128, 1024])
    wv_d = din("wv", [L * 8, 128, 1024])
    wf_d = din("wf", [L, 128, 64])
    wo_d = din("wo", [L, 128, 8192])
    wg_d = din("wg", [L * NFC, 128, 1024])
    wu_d = din("wu", [L * NFC, 128, 1024])
    wd_d = din("wd", [L * NFC, 128, 1024])
    gall_d = din("gall", [9, D])
    bf_d = din("bfl", [1, 32])
    rb_d = din("rb", [32, 8])
    cst_d = din("cst", [128, 512])
    ident_d = din("ident", [128, 128])
    kaug_d = din("kaug", [2, 8, S])
    ohb_d = din("ohb", [33, EXTW])
    out_d = nc.dram_tensor("out", [S, D], F32, kind="ExternalOutput").ap()
    if dbg:
        dbg_d = nc.dram_tensor("dbg", [128, 16384], F32, kind="ExternalOutput").ap()
    ext_t = nc.dram_tensor("ext_scr", [8, EXTW], BF16)
    ts_t = nc.dram_tensor("ts_scr", [8, 128, S], BF16)
    cpd_t = nc.dram_tensor("cpd_scr", [8, 3, S], BF16)
    ext_d = ext_t.ap()
    ts_d = ts_t.ap()
    cpd_d = cpd_t.ap()
    sum_d = nc.dram_tensor("sum_scr", [2, 512], F32).ap()

    x_v = x_d.rearrange("(i p) d -> i p d", p=128)
    out_v = out_d.rearrange("(i p) d -> i p d", p=128)

    ARENA_BYTES = 81344
    NSEM = 100
    with contextlib.ExitStack() as es:
        Xt = es.enter_context(nc.sbuf_tensor("X", [128, NT * D], F32))
        HTt = es.enter_context(nc.sbuf_tensor("HT", [128, 8 * S], BF16))
        YTt = es.enter_context(nc.sbuf_tensor("YT", [128, 8 * S], BF16))
        ARt = es.enter_context(nc.sbuf_tensor("AR", [128, ARENA_BYTES], U8))
        PSt = es.enter_context(nc.psum_tensor("PS", [128, 4096], F32))
        sems = [es.enter_context(nc.semaphore("s%d" % i)) for i in range(NSEM)]
        block = es.enter_context(nc.Block())
        sc = Sched(sems)

        X = Xt[:].rearrange("p (i d) -> p i d", i=NT)
        HT = HTt[:].rearrange("p (c t) -> p c t", c=8)
        YT = YTt[:].rearrange("p (c t) -> p c t", c=8)

        apos = [0]

        def carve(nbytes, dtype):
            off = apos[0]
            apos[0] += (nbytes + 63) // 64 * 64
            assert apos[0] <= ARENA_BYTES, apos[0]
            return ARt[:, off:off + nbytes].bitcast(dtype)

        QT = [None, None]
        KT = [None, None]
        QT[0] = carve(4096, BF16)
        KT[0] = carve(4096, BF16)
        STRIP = carve(8192, F32)
        WOv = ARt[:, 0:16384].bitcast(BF16)
        STG = [ARt[:, 12288 + 1024 * k_:12288 + 1024 * (k_ + 1)].bitcast(BF16) for k_ in range(4)]
        STGI = [0]
        QT[1] = carve(4096, BF16)
        KT[1] = carve(4096, BF16)
        TMPr = [carve(2048, F32), carve(2048, F32)]
        WDv = [ARt[:, 16384 + 2048 * s_:16384 + 2048 * (s_ + 1)].bitcast(BF16) for s_ in range(6)]
        PT = [carve(2048, BF16) for _ in range(3)]
        off_OSB = apos[0]
        OSB = carve(2048, F32)
        BC = carve(2048, F32)
        Gn = ARt[:, off_OSB:off_OSB + 4096].bitcast(F32)
        OSB2 = [OSB, carve(2048, F32)]
        BC2 = [BC, carve(2048, F32)]
        VA = [carve(NT * 65 * 2, BF16).rearrange("p (j c) -> p j c", c=65) for _ in range(2)]
        VB = [carve(NT * 128 * 2, BF16).rearrange("p (j c) -> p j c", c=128) for _ in range(2)]
        WQK = carve(2048, BF16)
        WV = carve(2048, BF16)
        WG = [carve(2048, BF16) for _ in range(2)]
        WU = [carve(2048, BF16) for _ in range(2)]
        WF = carve(128, BF16)
        IDENT = carve(256, BF16)
        CST = carve(2048, F32)
        ONES = carve(512, F32)
        CM = carve(256, BF16)
        BFT = carve(128, F32)
        SS = carve(64, F32)
        RSTD = carve(64, F32)
        RBA = carve(32, F32)
        GT = [carve(512, F32) for _ in range(4)]
        MX = carve(64, F32)
        MVB = carve(256, BF16)
        KM = carve(32, F32)
        KMH = carve(16, BF16)
        KML = carve(16, BF16)
        KMR = carve(32, F32)
        FL = carve(512, F32)
        LOGF = carve(512, F32)
        PSX = carve(512, F32)
        NEGC = carve(512, F32)
        CR = carve(2048, F32)
        CPS = carve(3072, BF16)
        ESTRIP = STRIP.bitcast(BF16)[:, 0:S]
        JB = carve(256, BF16)
        Jm = CST[:, 0:128]
        Um = CST[:, 128:256]
        PM = CST[:, 256:384]
        NOTOWN = CST[:, 384:512]

        def psb(b, dtype=F32):
            v = PSt[:, 512 * b:512 * (b + 1)]
            return v if dtype == F32 else v.bitcast(dtype)

        Xb = [[Buf("X%d_%d" % (i, h)) for h in range(2)] for i in range(NT)]
        HTb = [Buf("HT%d" % i) for i in range(NT)]
        YTb = [[Buf("YT%d_%d" % (c, g)) for g in range(4)] for c in range(8)]
        PSb = [Buf("PS%d" % b) for b in range(8)]
        QTd = [Buf("QTd0"), Buf("QTd1")]
        QTa = [Buf("QTa0"), Buf("QTa1")]
        KTd = [Buf("KTd0"), Buf("KTd1")]
        KTa = [Buf("KTa0"), Buf("KTa1")]
        STRIPb = Buf("STRIP")
        TMPb = [Buf("TMP0"), Buf("TMP1")]
        PTb = [Buf("PT%d" % i) for i in range(3)]
        OSBb = Buf("OSB")
        BCb = Buf("BC")
        OSB2b = [OSBb, Buf("OSB1")]
        BC2b = [BCb, Buf("BC1")]
        SUMb = [Buf("SUM0"), Buf("SUM1")]
        Vb = [Buf("Vm"), Buf("Vf")]
        WQKb = Buf("WQK")
        WVb = Buf("WV")
        WGb = [Buf("WG0"), Buf("WG1")]
        WUb = [Buf("WU0"), Buf("WU1")]
        WFb = Buf("WF")
        WOb = Buf("WO")
        WDb = [Buf("WD%d" % i) for i in range(6)]
        IDENTb = Buf("IDENT")
        CSTb = Buf("CST")
        ONESb = Buf("ONES")
        CMb = Buf("CM")
        BFTb = Buf("BFT")
        SSb = Buf("SS")
        SSt = [Buf("SSt%d" % i) for i in range(NT)]
        RSTDt = [Buf("RSTDt%d" % i) for i in range(NT)]
        RSTDb = Buf("RSTD")
        RBAb = Buf("RBA")
        GTb = [Buf("GT%d" % i) for i in range(4)]
        MXb = Buf("MX")
        MVBb = Buf("MVB")
        KMb = Buf("KM")
        KMHb = Buf("KMH")
        FLb = Buf("FL")
        LOGFb = Buf("LOGF")
        PSXb = Buf("PSX")
        NEGCb = Buf("NEGC")
        CRb = Buf("CR")
        CPSb = Buf("CPS")
        EXTb = Buf("EXT")
        TSb = [Buf("TS%d" % h) for h in range(8)]
        CPDb = Buf("CPD")
        OUTb = [Buf("OUT%d" % i) for i in range(4)]
        DBGb = Buf("DBG")
        STGb = [Buf("STG%d" % k_) for k_ in range(4)]
        WO_OVER = [QTd[0], QTa[0], KTd[0], KTa[0], STRIPb] + STGb
        WD_OVER = [[QTd[1], QTa[1]], [QTd[1], QTa[1]], [KTd[1], KTa[1]], [KTd[1], KTa[1]],
                   [TMPb[0]], [TMPb[1]]]

        sc.dma("sp", CST, cst_d, CSTb)
        sc.dma("pool", IDENT, ident_d, IDENTb)
        sc.dma("sp", BFT[:, 0:32], bf_d.to_broadcast([128, 32]), BFTb)
        for i in range(NT):
            sc.dma("sp", X[:, i, :], x_v[i], Xb[i][0], also_writes=[Xb[i][1]])
        sc.op("dve", lambda e: e.memset(ONES, 1.0), writes=[ONESb])
        sc.op("dve", lambda e: e.tensor_scalar(CM, Um, -1.0, BIG, ALU.add, ALU.mult),
              reads=[CSTb], writes=[CMb])
        for t in range(2):
            sc.op("dve", lambda e, t=t: e.memset(QT[t][64:72, :], 0.0), writes=[QTa[t]])
            sc.op("dve", lambda e, t=t: e.memset(VB[t], 0.0), writes=[Vb[t]])
            sc.op("dve", lambda e, t=t: e.memset(VB[t][:, :, 0:1], 1.0), writes=[Vb[t]])
            sc.op("dve", lambda e, t=t: e.memset(VA[t][:, :, 64:65], 1.0), writes=[Vb[t]])
        sc.op("dve", lambda e: e.memset(PSX, 0.0), writes=[PSXb])

        def t5_prologue():
            sc.dma("sp", RBA[0:32, 0:8], rb_d, RBAb)
            sc.op("dve", lambda e: e.memset(RBA[32:33, 0:8], 1.0), writes=[RBAb])
            sc.op("dve", lambda e: e.tensor_copy(out=JB, in_=Jm), reads=[CSTb], writes=[CSTb])
            osbb = OSB.bitcast(BF16)
            for ch in range(5):
                c0 = 512 * ch
                w = min(512, EXTW - c0)
                sc.dma("sp", TMPr[ch % 2][0:33, 0:w], ohb_d[:, c0:c0 + w], TMPb[ch % 2])
                sc.pe_group([lambda e, w=w, ch=ch: e.matmul(psb(7)[0:8, 0:w], lhsT=RBA[0:33, 0:8],
                                                            rhs=TMPr[ch % 2][0:33, 0:w], start=True, stop=True)],
                            reads=[RBAb, TMPb[ch % 2]], writes=[PSb[7]])
                sc.op("act", lambda e, w=w: e.activation(out=osbb[0:8, 0:w], in_=psb(7)[0:8, 0:w], func=AF.Exp),
                      reads=[PSb[7]], writes=[OSBb])
                sc.dma("sp", ext_d[:, c0:c0 + w], osbb[0:8, 0:w], EXTb, reads=[OSBb])
            TSTs = [ARt[:, 16384:16384 + 4096].bitcast(BF16), ARt[:, 20480:20480 + 4096].bitcast(BF16),
                    ARt[:, 0:4096].bitcast(BF16), ARt[:, 4096:8192].bitcast(BF16)]
            TSTbs = [[QTd[1], QTa[1]], [KTd[1], KTa[1]], [QTd[0], QTa[0]], [KTd[0], KTa[0]]]
            for h in range(8):
                TST, TSTb = TSTs[h % 4], TSTbs[h % 4]
                tap = bass.AP(ext_t, h * EXTW, [[1, 128], [1, S]])
                sc.dma("sp", TST, tap, TSTb[0], reads=[EXTb], also_writes=TSTb[1:])
                for ch in range(4):
                    k = ch % 2
                    eb = TMPr[k].bitcast(BF16)[:, 0:512]
                    sc.pe_group([lambda e, ch=ch, TST=TST: e.matmul(psb(5 + ch % 2), lhsT=JB, rhs=TST[:, 512 * ch:512 * ch + 512],
                                                            start=True, stop=True)],
                                reads=[CSTb] + TSTb, writes=[PSb[5 + ch % 2]])
                    if ch % 2 == 0:
                        sc.op("act", lambda e, ch=ch, eb=eb: e.activation(out=eb, in_=psb(5 + ch % 2), func=AF.Copy),
                              reads=[PSb[5 + ch % 2]], writes=[TMPb[k]])
                    else:
                        sc.op("dve", lambda e, ch=ch, eb=eb: e.tensor_copy(out=eb, in_=psb(5 + ch % 2)),
                              reads=[PSb[5 + ch % 2]], writes=[TMPb[k]])
                    sc.dma("sp", ts_d[h, :, 512 * ch:512 * ch + 512], eb, TSb[h], reads=[TMPb[k]])
            for t in range(2):
                sc.op("dve", lambda e, t=t: e.memset(QT[t][64:72, :], 0.0), writes=[QTa[t]])


        Gv = STRIP[:, 0:D]

        def ss_begin():
            sc.op("dve", lambda e: e.memset(SS[:, 0:NT], 0.0), writes=[SSb])

        def ss_tile(i):
            junk = OSB.bitcast(BF16)
            sc.op("act", lambda e, i=i: e.activation(out=junk, in_=X[:, i, :], func=AF.Square,
                                                     accum_out=SS[:, i:i + 1]),
                  reads=[Xb[i][0], Xb[i][1]], writes=[OSBb, SSb])

        def emit_norm(gidx, final=False, ss_done=False):
            sc.dma("sp", Gv, gall_d[gidx:gidx + 1, :].to_broadcast([128, D]), STRIPb)
            if not ss_done:
                ss_begin()
                for i in range(NT):
                    ss_tile(i)
            sc.op("dve", lambda e: e.tensor_scalar(RSTD[:, 0:NT], SS[:, 0:NT], 1.0 / D, EPS, ALU.mult, ALU.add),
                  reads=[SSb], writes=[RSTDb])
            sc.op("act", lambda e: e.activation(out=RSTD[:, 0:NT], in_=RSTD[:, 0:NT], func=AF.Sqrt),
                  reads=[], writes=[RSTDb])
            sc.op("dve", lambda e: e.reciprocal(RSTD[:, 0:NT], RSTD[:, 0:NT]),
                  reads=[], writes=[RSTDb])
            for i in range(NT):
                if final:
                    for hf in range(2):
                        sc.op("dve", lambda e, i=i, hf=hf: e.scalar_tensor_tensor(
                            out=X[:, i, 512 * hf:512 * hf + 512], in0=X[:, i, 512 * hf:512 * hf + 512],
                            scalar=RSTD[:, i:i + 1], in1=Gv[:, 512 * hf:512 * hf + 512],
                            op0=ALU.mult, op1=ALU.mult),
                            reads=[RSTDb, STRIPb], writes=[Xb[i][hf]])
                    sc.dma("sp", out_v[i], X[:, i, :], OUTb[i % 4], reads=[Xb[i][0], Xb[i][1]])
                    continue
                k = i % 2
                hn = TMPr[k].bitcast(BF16)
                sc.op("dve", lambda e, i=i, hn=hn: e.scalar_tensor_tensor(
                    out=hn, in0=X[:, i, :], scalar=RSTD[:, i:i + 1], in1=Gv, op0=ALU.mult, op1=ALU.mult),
                    reads=[Xb[i][0], Xb[i][1], RSTDb, STRIPb], writes=[TMPb[k]])
                bank = 5 + i % 2
                pst = psb(bank, BF16).rearrange("p (c t) -> p c t", c=8)
                sc.pe_group([lambda e, c=c, hn=hn, pst=pst: e.transpose(pst[:, c, :], hn[:, 128 * c:128 * c + 128], IDENT)
                             for c in range(8)],
                            reads=[TMPb[k], IDENTb], writes=[PSb[bank]])
                sc.op("act", lambda e, i=i, pst=pst: e.activation(out=HT[:, :, 128 * i:128 * i + 128], in_=pst, func=AF.Copy),
                      reads=[PSb[bank]], writes=[HTb[i]])

        def make_norm(gidx, final):
            junk = OSB2[1].bitcast(BF16)
            gb = [OSB2b[0], BC2b[0]]

            def begin():
                sc.dma("sp", Gn, gall_d[gidx:gidx + 1, :].to_broadcast([128, D]), gb[0], also_writes=gb[1:])
                sc.op("dve", lambda e: e.memset(SS[:, 0:NT], 0.0), writes=SSt)

            def stA(i):
                sc.op("act", lambda e: e.activation(out=junk, in_=X[:, i, :], func=AF.Square, accum_out=SS[:, i:i + 1]),
                      reads=[Xb[i][0], Xb[i][1]], writes=[OSB2b[1], SSt[i]])

            def stB(i):
                sc.op("dve", lambda e: e.tensor_scalar(RSTD[:, i:i + 1], SS[:, i:i + 1], 1.0 / D, EPS, ALU.mult, ALU.add),
                      reads=[SSt[i]], writes=[RSTDt[i]])
                sc.op("act", lambda e: e.activation(out=RSTD[:, i:i + 1], in_=RSTD[:, i:i + 1], func=AF.Sqrt),
                      reads=[], writes=[RSTDt[i]])
                sc.op("dve", lambda e: e.reciprocal(RSTD[:, i:i + 1], RSTD[:, i:i + 1]), reads=[], writes=[RSTDt[i]])

            def stC(i):
                if final:
                    for hf in range(2):
                        sc.op("dve", lambda e, hf=hf: e.scalar_tensor_tensor(
                            out=X[:, i, 512 * hf:512 * hf + 512], in0=X[:, i, 512 * hf:512 * hf + 512],
                            scalar=RSTD[:, i:i + 1], in1=Gn[:, 512 * hf:512 * hf + 512], op0=ALU.mult, op1=ALU.mult),
                            reads=[RSTDt[i]] + gb, writes=[Xb[i][hf]])
                    sc.dma("sp", out_v[i], X[:, i, :], OUTb[i % 4], reads=[Xb[i][0], Xb[i][1]])
                    return
                hn = PT[i % 2]
                sc.op("dve", lambda e: e.scalar_tensor_tensor(
                    out=hn, in0=X[:, i, :], scalar=RSTD[:, i:i + 1], in1=Gn, op0=ALU.mult, op1=ALU.mult),
                    reads=[Xb[i][0], Xb[i][1], RSTDt[i]] + gb, writes=[PTb[i % 2]])

            def stD(i):
                if final:
                    return
                hn = PT[i % 2]
                bank = i % 2
                pst = psb(bank, BF16).rearrange("p (c t) -> p c t", c=8)
                sc.pe_group([lambda e, c=c: e.transpose(pst[:, c, :], hn[:, 128 * c:128 * c + 128], IDENT)
                             for c in range(8)],
                            reads=[PTb[i % 2], IDENTb], writes=[PSb[bank]])

            def stE(i):
                if final:
                    return
                bank = i % 2
                pst = psb(bank, BF16).rearrange("p (c t) -> p c t", c=8)
                sc.op("act", lambda e: e.activation(out=HT[:, :, 128 * i:128 * i + 128], in_=pst, func=AF.Copy),
                      reads=[PSb[bank]], writes=[HTb[i]])
            stages = [stA, stB, stC, stD, stE]

            def after_tile(i):
                for k_, st in enumerate(stages):
                    if 0 <= i - k_ < NT:
                        st(i - k_)

            def flush():
                for i in range(NT, NT + len(stages) - 1):
                    after_tile(i)
            return begin, after_tile, flush

        def dump(ap2d, reads, ncols):
            sc.dma("pool", dbg_d[:, 0:ncols], ap2d, DBGb, reads=reads)

        def finish():
            sc.wait_all("sp", OUTb + [DBGb])

        def load_wqk(l, hh):
            sc.dma("pool", WQK, wqk_d[l * 16 + hh], WQKb)

        def load_wv(l, pp):
            sc.dma("pool", WV, wv_d[l * 8 + pp], WVb)

        head_order = []
        for i in range(8):
            head_order.append((0, i))
            head_order.append((1, i))

        def split_group(steps, fns, reads, writes, per_step, after=None):
            n = len(fns)
            pos = list(range(0, n, per_step))
            holder = {}
            for si, p0 in enumerate(pos):
                part = fns[p0:p0 + per_step]
                first = si == 0
                last = si == len(pos) - 1

                def st(part=part, first=first, last=last):
                    if first:
                        holder["h"] = sc.pe_open(reads=reads, writes=writes)
                    for f in (part[:-1] if last else part):
                        sc.pe_part(f)
                    if last:
                        sc.pe_close(holder["h"], part[-1])
                        if after is not None:
                            after()
                steps.append(st)

        def fox_layer_chunks(l, fbank=7):
            ch = []
            p7 = psb(fbank)
            v3 = lambda a: a[:, 0:128].rearrange("p (i h) -> p i h", h=8)
            ch.append(lambda: sc.dma("pool", WF[:, 0:64], wf_d[l], WFb))
            for i0 in range(0, NT, 2):
                def c_f(i0=i0):
                    for i in (i0, i0 + 1):
                        sc.pe_group([lambda e, i=i, kc=kc: e.matmul(p7[:, 8 * i:8 * i + 8], lhsT=HT[:, kc, 128 * i:128 * i + 128],
                                                                    rhs=WF[:, 8 * kc:8 * kc + 8], start=(kc == 0), stop=(kc == 7))
                                     for kc in range(8)],
                                    reads=[HTb[i], WFb], writes=[PSb[fbank]])
                ch.append(c_f)

            def c_lf():
                bfb = BFT[:, 8 * l:8 * l + 8].unsqueeze(1).to_broadcast([128, NT, 8])
                sc.op("dve", lambda e: e.tensor_tensor(out=v3(FL), in0=v3(p7), in1=bfb, op=ALU.add),
                      reads=[PSb[fbank], BFTb], writes=[FLb])
                sc.op("act", lambda e: e.activation(out=FL[:, 0:128], in_=FL[:, 0:128], func=AF.Exp, scale=-1.0),
                      reads=[], writes=[FLb])
                sc.op("act", lambda e: e.activation(out=FL[:, 0:128], in_=FL[:, 0:128], func=AF.Ln, bias=1.0),
                      reads=[], writes=[FLb])
            ch.append(c_lf)
            ch.append(lambda: None)
            ch.append(lambda: None)

            def c_px0():
                sc.op("dve", lambda e: e.tensor_scalar_mul(LOGF[:, 0:128], FL[:, 0:128], -1.0),
                      reads=[FLb], writes=[LOGFb])
            ch.append(c_px0)
            for i in range(1, NT):
                ch.append(lambda i=i: sc.op("dve", lambda e: e.tensor_tensor(
                    out=PSX[:, 8 * i:8 * i + 8], in0=PSX[:, 8 * i - 8:8 * i], in1=LOGF[:, 8 * i - 8:8 * i], op=ALU.add),
                    reads=[LOGFb], writes=[PSXb]))
            fns = []
            for i in range(NT):
                fns.append(lambda e, i=i: e.matmul(p7[:, 8 * i:8 * i + 8], lhsT=Um, rhs=LOGF[:, 8 * i:8 * i + 8],
                                                   start=True, stop=False))
                fns.append(lambda e, i=i: e.matmul(p7[:, 8 * i:8 * i + 8], lhsT=ONES[:, 0:128], rhs=PSX[:, 8 * i:8 * i + 8],
                                                   start=False, stop=True))
            split_group(ch, fns, [CSTb, ONESb, LOGFb, PSXb], [PSb[fbank]], 8)
            ch.append(lambda: None)
            ch.append(lambda: sc.op("dve", lambda e: e.tensor_scalar_mul(NEGC[:, 0:128], p7[:, 0:128], -1.0),
                                    reads=[PSb[fbank]], writes=[NEGCb]))
            cps3 = CPS[0:8, 0:1536].rearrange("p (k t) -> p k t", k=3)
            for r in range(4):
                fns = []
                for ti in range(4):
                    i = 4 * r + ti
                    fns.append(lambda e, i=i, ti=ti: e.matmul(p7[0:8, 128 * ti:128 * ti + 128], lhsT=LOGF[:, 8 * i:8 * i + 8],
                                                              rhs=Um, start=True, stop=False))
                    fns.append(lambda e, i=i, ti=ti: e.matmul(p7[0:8, 128 * ti:128 * ti + 128], lhsT=PSX[:, 8 * i:8 * i + 8],
                                                              rhs=ONES[:, 0:128], start=False, stop=True))
                split_group(ch, fns, [CSTb, ONESb, LOGFb, PSXb], [PSb[fbank]], 2)
                ch.append(lambda: None)

                def c_r2(r=r):
                    sc.op("dve", lambda e: e.tensor_copy(out=cps3[:, 0, :], in_=p7[0:8, :]), reads=[PSb[fbank]], writes=[CPSb])
                    sc.op("dve", lambda e: e.tensor_tensor(out=CR[0:8, :], in0=p7[0:8, :], in1=cps3[:, 0, :], op=ALU.subtract),
                          reads=[PSb[fbank]], writes=[CRb, CPSb])
                ch.append(c_r2)

                def c_r3(r=r):
                    sc.op("dve", lambda e: e.tensor_copy(out=cps3[:, 1, :], in_=CR[0:8, :]), reads=[], writes=[CPSb, CRb])
                    sc.op("dve", lambda e: e.tensor_tensor(out=CR[0:8, :], in0=CR[0:8, :], in1=cps3[:, 1, :], op=ALU.subtract),
                          reads=[], writes=[CRb, CPSb])
                ch.append(c_r3)

                def c_r4(r=r):
                    sc.op("dve", lambda e: e.tensor_copy(out=cps3[:, 2, :], in_=CR[0:8, :]), reads=[], writes=[CPSb, CRb])
                    sc.dma("sp", cpd_d[:, :, 512 * r:512 * r + 512], cps3, CPDb, reads=[CPSb])
                ch.append(c_r4)
            return ch

        def prep_chunks(l, idx):
            typ, hi = head_order[idx]
            hh = hi + 8 * typ
            par = typ
            ch = []

            def c_aug():
                sc.dma("pool", KT[par][64:72, :], kaug_d[typ], KTa[par])
                if typ == 0:
                    sc.dma("sp", ESTRIP, ts_d[hi], STRIPb, reads=[TSb[hi]])
            ch.append(c_aug)
            for tg in range(4):
                bank = 6 + tg % 2
                pp_ = psb(bank)
                fns = [lambda e, kc=kc, pp_=pp_, tg=tg: e.matmul(
                    pp_, lhsT=WQK[:, 128 * kc:128 * kc + 128],
                    rhs=HT[:, kc, 512 * tg:512 * tg + 512], start=(kc == 0), stop=(kc == 7))
                    for kc in range(8)]
                split_group(ch, fns, [WQKb] + HTb[4 * tg:4 * tg + 4], [PSb[bank]], 1)
                sgi = STGI[0] % 4
                STGI[0] += 1

                def ev(pp_=pp_, tg=tg, bank=bank, sgi=sgi):
                    sc.op("dve", lambda e: e.tensor_scalar_mul(
                        QT[par][0:64, 512 * tg:512 * tg + 512], pp_[0:64, :], 0.125),
                        reads=[PSb[bank]], writes=[QTd[par]])
                    sc.op("dve", lambda e: e.tensor_copy(out=STG[sgi][64:128, :], in_=pp_[64:128, :]),
                          reads=[PSb[bank]], writes=[STGb[sgi]])
                    sc.dma("sp", KT[par][0:64, 512 * tg:512 * tg + 512], STG[sgi][64:128, :], KTd[par], reads=[STGb[sgi]])
                ch.append(("lag", ev))

            def c_wq():
                if idx + 1 < 16:
                    t2, h2 = head_order[idx + 1]
                    load_wqk(l, h2 + 8 * t2)
                elif l + 1 < n_layers:
                    load_wqk(l + 1, 0)
            ch.append(("lag", c_wq))
            if hi % 2 == 0:
                for q4 in range(4):
                    bank = 6 + q4 % 2
                    pv = psb(bank)
                    fns = []
                    for ti in range(4):
                        i = 4 * q4 + ti
                        for kc in range(8):
                            fns.append(lambda e, i=i, ti=ti, kc=kc, pv=pv: e.matmul(
                                pv[:, 128 * ti:128 * ti + 128], lhsT=HT[:, kc, 128 * i:128 * i + 128],
                                rhs=WV[:, 128 * kc:128 * kc + 128], start=(kc == 0), stop=(kc == 7)))
                    split_group(ch, fns, [WVb] + HTb[4 * q4:4 * q4 + 4], [PSb[bank]], 4)
                    pv3 = pv.rearrange("p (t c) -> p t c", c=128)

                    def ev(q4=q4, pv3=pv3, bank=bank):
                        sc.op("dve", lambda e: e.tensor_copy(out=VA[typ][:, 4 * q4:4 * q4 + 4, 0:64], in_=pv3[:, :, 0:64]),
                              reads=[PSb[bank]], writes=[Vb[typ]])
                        sc.op("dve", lambda e: e.tensor_copy(out=VB[typ][:, 4 * q4:4 * q4 + 4, 64:128], in_=pv3[:, :, 64:128]),
                              reads=[PSb[bank]], writes=[Vb[typ]])
                    ch.append(("lag", ev))

                def c_wv():
                    nxt = None
                    for j2 in range(idx + 1, 16):
                        t2, h2 = head_order[j2]
                        if h2 % 2 == 0:
                            nxt = (l, h2 // 2 + 4 * t2)
                            break
                    if nxt is None and l + 1 < n_layers:
                        nxt = (l + 1, 0)
                    if nxt is not None:
                        load_wv(*nxt)
                ch.append(("lag", c_wv))
            ch.append(lambda: None)
            ch.append(lambda: None)
            ch.append(lambda: None)
            if typ == 1:
                ch.append(lambda: sc.dma("sp", QT[1][64:67, :], cpd_d[hi], QTa[1], reads=[CPDb]))
            else:
                p7 = psb(7)
                g0, g1, e1 = GT[0][:, 0:128], GT[1][:, 0:128], GT[2][:, 0:128]
                v3 = lambda a: a.rearrange("p (i n) -> p i n", n=8)
                mxb = lambda: MX[:, 0:16].unsqueeze(2).to_broadcast([128, NT, 8])

                def c_g1():
                    sc.op("dve", lambda e: e.tensor_reduce(out=KM[0:64, 0:8], in_=KT[0][0:64, :].rearrange("p (n s) -> p n s", s=256),
                                                           axis=AX.X, op=ALU.add),
                          reads=[KTd[0]], writes=[KMb])
                ch.append(c_g1)
                ch.append(lambda: None)
                ch.append(lambda: None)

                def c_g1b():
                    sc.op("dve", lambda e: e.tensor_copy(out=KMH[0:64, 0:8], in_=KM[0:64, 0:8]), reads=[], writes=[KMHb, KMb])
                    sc.op("dve", lambda e: e.tensor_tensor(out=KMR[0:64, 0:8], in0=KM[0:64, 0:8], in1=KMH[0:64, 0:8], op=ALU.subtract),
                          reads=[], writes=[KMHb, KMb])
                    sc.op("dve", lambda e: e.tensor_copy(out=KML[0:64, 0:8], in_=KMR[0:64, 0:8]), reads=[], writes=[KMHb, KMb])
                ch.append(c_g1b)
                for _ in range(4):
                    ch.append(lambda: None)
                fns = []
                for i in range(NT):
                    fns.append(lambda e, i=i: e.matmul(p7[:, 8 * i:8 * i + 8], lhsT=QT[0][0:64, 128 * i:128 * i + 128],
                                                       rhs=KMH[0:64, 0:8], start=True, stop=False))
                    fns.append(lambda e, i=i: e.matmul(p7[:, 8 * i:8 * i + 8], lhsT=QT[0][0:64, 128 * i:128 * i + 128],
                                                       rhs=KML[0:64, 0:8], start=False, stop=True))
                split_group(ch, fns, [QTd[0], KMHb], [PSb[7]], 8)
                for _ in range(4):
                    ch.append(lambda: None)

                def c_g3a():
                    sc.op("dve", lambda e: e.tensor_tensor(out=g0, in0=p7[:, 0:128], in1=PM, op=ALU.add),
                          reads=[PSb[7], CSTb], writes=[GTb[0]])
                ch.append(c_g3a)
                src = g0
                srcb = GTb[0]
                for rnd in range(2):
                    dst = g1 if rnd == 0 else e1
                    dstb = GTb[1] if rnd == 0 else GTb[2]

                    def c_rnd(src=src, srcb=srcb, dst=dst, dstb=dstb):
                        sc.op("dve", lambda e: e.tensor_reduce(out=MX[:, 0:16], in_=v3(src), axis=AX.X, op=ALU.max),
                              reads=[srcb], writes=[MXb])
                        sc.op("dve", lambda e: e.tensor_tensor(out=v3(GT[3][:, 0:128]), in0=v3(src), in1=mxb(), op=ALU.is_ge),
                              reads=[srcb, MXb], writes=[GTb[3]])
                        sc.op("dve", lambda e: e.scalar_tensor_tensor(out=dst, in0=GT[3][:, 0:128], scalar=-BIG, in1=src,
                                                                      op0=ALU.mult, op1=ALU.add),
                              reads=[srcb, GTb[3]], writes=[dstb])
                    ch.append(c_rnd)
                    src = dst
                    srcb = dstb

                def c_g3b(src=src, srcb=srcb):
                    sc.op("dve", lambda e: e.tensor_reduce(out=MX[:, 0:16], in_=v3(src), axis=AX.X, op=ALU.max),
                          reads=[srcb], writes=[MXb])
                    sc.op("dve", lambda e: e.tensor_tensor(out=v3(GT[3][:, 0:128]), in0=v3(g0), in1=mxb(), op=ALU.is_ge),
                          reads=[GTb[0], MXb], writes=[GTb[3]])
                    sc.op("dve", lambda e: e.tensor_scalar(GT[3][:, 0:128], GT[3][:, 0:128], -1.0, BIG, ALU.add, ALU.mult),
                          reads=[], writes=[GTb[3]])
                    sc.op("dve", lambda e: e.tensor_tensor(out=MVB[:, 0:128], in0=GT[3][:, 0:128], in1=NOTOWN, op=ALU.mult),
                          reads=[GTb[3], CSTb], writes=[MVBb])
                ch.append(c_g3b)
                for _ in range(4):
                    ch.append(lambda: None)
                mt = CPS[0:8, 0:1024]
                for half in range(2):
                    def c_g4(half=half):
                        pst = psb(7, BF16)
                        sc.pe_group([lambda e, t=t: e.transpose(
                            pst[0:8, 128 * t:128 * t + 128], MVB[:, 8 * (8 * half + t):8 * (8 * half + t) + 8], IDENT)
                            for t in range(8)],
                            reads=[MVBb, IDENTb], writes=[PSb[7]])
                    ch.append(c_g4)
                    for _ in range(3):
                        ch.append(lambda: None)

                    def c_g5(half=half):
                        pst = psb(7, BF16)
                        sc.op("dve", lambda e: e.tensor_copy(out=mt, in_=pst[0:8, 0:1024]), reads=[PSb[7]], writes=[CPSb])
                        sc.dma("sp", QT[0][64:72, 1024 * half:1024 * half + 1024], mt, QTa[0], reads=[CPSb])
                    ch.append(c_g5)
            out = []
            lagq = []
            for c in ch:
                for q in lagq:
                    q[0] += 1
                while lagq and lagq[0][0] >= 2:
                    f = lagq.pop(0)[1]
                    out.append(f)
                if isinstance(c, tuple):
                    lagq.append([0, c[1]])
                else:
                    out.append(c)
            for q in lagq:
                out.append(q[1])
            return out

        deferred = []

        def tick_deferred(flush=False):
            for d_ in deferred:
                d_[0] -= 1
            i_ = 0
            while i_ < len(deferred):
                if flush or deferred[i_][0] <= 0:
                    deferred.pop(i_)[1]()
                else:
                    i_ += 1

        def make_head(l, idx):
            typ, hi = head_order[idx]
            par = typ
            Bl = hi % 2
            chunk = hi // 2 + 4 * typ
            mv = 128 if Bl else 65
            tiles = [(gp, j) for gp in range(2) for j in range(8 * (gp + 1))]
            nT = len(tiles)

            def geom(n):
                gp, j = tiles[n]
                base = 1024 * gp
                c0 = max(base, 128 * j)
                c1 = base + 1024
                if c0 < base + 512:
                    pieces = [(c0, base + 512), (base + 512, c1)]
                else:
                    pieces = [(c0, c1)]
                return gp, j, base, c0, c1, pieces

            def sset(n):
                st_ = n % 2
                return PSt[:, 1024 * st_:1024 * st_ + 1024], [PSb[2 * st_], PSb[2 * st_ + 1]]

            def qk(n):
                gp, j, base, c0, c1, pieces = geom(n)
                SSv, sb = sset(n)
                diag = (typ == 1 and j >= 8 * gp)
                fns = []
                for pi, (cs, ce) in enumerate(pieces):
                    fns.append(lambda e, cs=cs, ce=ce, pi=pi: e.matmul(
                        SSv[:, cs - base:ce - base], lhsT=KT[par][0:72, 128 * j:128 * j + 128], rhs=QT[par][0:72, cs:ce],
                        start=True, stop=not (diag and pi == 0)))
                    if diag and pi == 0:
                        fns.append(lambda e: e.matmul(SSv[:, c0 - base:c0 - base + 128], lhsT=IDENT, rhs=CM[:, 0:128],
                                                      start=False, stop=True))
                rd = [KTd[par], KTa[par], QTd[par], QTa[par]]
                if diag:
                    rd += [IDENTb, CMb]
                sc.pe_group(fns, reads=rd, writes=sb)

            def soft(n):
                gp, j, base, c0, c1, pieces = geom(n)
                SSv, sb = sset(n)
                pk = n % 3
                lo = c0 - base
                if typ == 0:
                    off = c0 - 128 * j
                    w = c1 - c0
                    sc.op("act", lambda e: e.activation(out=PT[pk][:, lo:1024], in_=SSv[:, lo:1024], func=AF.Exp),
                          reads=sb, writes=[PTb[pk]])
                    sc.op("dve", lambda e: e.tensor_tensor(out=PT[pk][:, lo:1024], in0=PT[pk][:, lo:1024],
                                                           in1=ESTRIP[:, off:off + w], op=ALU.mult),
                          reads=[STRIPb], writes=[PTb[pk]])
                else:
                    sc.op("act", lambda e: e.activation(out=PT[pk][:, lo:1024], in_=SSv[:, lo:1024], func=AF.Exp,
                                                        bias=NEGC[:, 8 * j + hi:8 * j + hi + 1]),
                          reads=sb + [NEGCb], writes=[PTb[pk]])

            def pv(n):
                gp, j, base, c0, c1, pieces = geom(n)
                pk = n % 3
                lh = VB[typ][:, j, :] if Bl else VA[typ][:, j, 0:65]
                for (cs, ce) in pieces:
                    g = cs // 512
                    ob = 4 + g % 2
                    po = psb(ob)
                    sc.pe_group([lambda e, cs=cs, ce=ce, g=g, po=po: e.matmul(
                        po[0:mv, cs - 512 * g:ce - 512 * g], lhsT=lh, rhs=PT[pk][:, cs - base:ce - base],
                        start=(j == 0), stop=(j == 4 * g + 3))],
                        reads=[Vb[typ], PTb[pk]], writes=[PSb[ob]])
                    if j == 4 * g + 3:
                        epilogue(g, ob, po)

            def epilogue(g, ob, po):
                rows = slice(0, 128) if Bl else slice(0, 65)
                orow = slice(64, 128) if Bl else slice(0, 64)
                srow = 0 if Bl else 64
                kb = (4 * idx + g) % 2
                osb, bc = OSB2[kb], BC2[kb]
                def st1():
                    sc.op("dve", lambda e: e.tensor_copy(out=osb[rows, :], in_=po[rows, :]),
                          reads=[PSb[ob]], writes=[OSB2b[kb]])

                def st2():
                    sc.op("act", lambda e: e.activation(out=osb[srow:srow + 1, :], in_=osb[srow:srow + 1, :], func=AF.Ln),
                          reads=[], writes=[OSB2b[kb]])
                    sc.op("act", lambda e: e.activation(out=osb[srow:srow + 1, :], in_=osb[srow:srow + 1, :], func=AF.Exp, scale=-1.0),
                          reads=[], writes=[OSB2b[kb]])
                    sc.dma("sp", sum_d[kb:kb + 1, :], osb[srow:srow + 1, :], SUMb[kb], reads=[OSB2b[kb]])
                    sc.dma("sp", bc[orow, :], sum_d[kb:kb + 1, :].to_broadcast([64, 512]), BC2b[kb], reads=[SUMb[kb]])

                def st3():
                    sc.op("dve", lambda e: e.tensor_tensor(
                        out=YT[orow, chunk, 512 * g:512 * g + 512], in0=osb[orow, :], in1=bc[orow, :], op=ALU.mult),
                        reads=[OSB2b[kb], BC2b[kb]], writes=[YTb[chunk][g]])
                deferred.append([1, st1])
                deferred.append([2, st2])
                deferred.append([6, st3])

            return nT, qk, soft, pv

        def outproj(l):
            wo3 = WOv.rearrange("p (c n) -> p c n", c=8)
            k = 0
            nb, nafter, nflush = make_norm(2 * l + 1, False)
            nb()
            for i in range(NT):
                if i >= 1:
                    nafter(i - 1)
                for hf in range(2):
                    bank = 5 + k % 2
                    k += 1
                    ps = psb(bank)
                    sc.pe_group([lambda e, c=c, i=i, hf=hf, ps=ps: e.matmul(
                        ps, lhsT=YT[:, c, 128 * i:128 * i + 128], rhs=wo3[:, c, 512 * hf:512 * hf + 512],
                        start=(c == 0), stop=(c == 7)) for c in range(8)],
                        reads=[WOb] + WO_OVER + [YTb[c][i // 4] for c in range(8)], writes=[PSb[bank]])
                    sc.op("dve", lambda e, i=i, hf=hf, ps=ps: e.tensor_tensor(
                        out=X[:, i, 512 * hf:512 * hf + 512], in0=X[:, i, 512 * hf:512 * hf + 512], in1=ps, op=ALU.add),
                        reads=[PSb[bank]], writes=[Xb[i][hf]])
            nafter(NT - 1)
            nflush()

        def load_gu(l, fc):
            sc.dma("pool", WG[fc % 2], wg_d[l * NFC + fc], WGb[fc % 2])
            sc.dma("pool", WU[fc % 2], wu_d[l * NFC + fc], WUb[fc % 2])

        def ffn(l):
            supers = [(0, 6), (6, 12), (12, 17), (17, 22)]
            kk = 0
            for (a, b) in supers:
                for fc in range(a, b):
                    fci = fc - a
                    if fc + 1 < NFC:
                        load_gu(l, fc + 1)
                    if fc == a:
                        for fc2 in range(a, b):
                            s_ = fc2 - a
                            sc.dma("pool", WDv[s_], wd_d[l * NFC + fc2], WDb[s_], also_writes=WD_OVER[s_])
                    sl = fc % 2
                    for tg in range(4):
                        bg = kk % 2
                        bu = 2 + kk % 2
                        pk = kk % 3
                        kk += 1
                        pg, pu = psb(bg), psb(bu)
                        sc.pe_group([lambda e, kc=kc, tg=tg, pg=pg, sl=sl: e.matmul(
                            pg, lhsT=WG[sl][:, 128 * kc:128 * kc + 128], rhs=HT[:, kc, 512 * tg:512 * tg + 512],
                            start=(kc == 0), stop=(kc == 7)) for kc in range(8)],
                            reads=[WGb[sl]] + HTb[4 * tg:4 * tg + 4], writes=[PSb[bg]])
                        sc.pe_group([lambda e, kc=kc, tg=tg, pu=pu, sl=sl: e.matmul(
                            pu, lhsT=WU[sl][:, 128 * kc:128 * kc + 128], rhs=HT[:, kc, 512 * tg:512 * tg + 512],
                            start=(kc == 0), stop=(kc == 7)) for kc in range(8)],
                            reads=[WUb[sl]] + HTb[4 * tg:4 * tg + 4], writes=[PSb[bu]])
                        sc.op("act", lambda e, pg=pg, pk=pk: e.activation(out=PT[pk][:, 0:512], in_=pg, func=AF.Silu),
                              reads=[PSb[bg]], writes=[PTb[pk]])
                        sc.op("dve", lambda e, pu=pu, pk=pk, fci=fci, tg=tg: e.tensor_tensor(
                            out=YT[:, fci, 512 * tg:512 * tg + 512], in0=PT[pk][:, 0:512], in1=pu, op=ALU.mult),
                            reads=[PTb[pk], PSb[bu]], writes=[YTb[fci][tg]])
                nch = b - a
                k = 0
                lastsc = (b == NFC)
                if lastsc:
                    ss_begin()
                for i in range(NT):
                    if lastsc and i >= 1:
                        ss_tile(i - 1)
                    for hf in range(2):
                        bank = 5 + k % 2
                        k += 1
                        ps = psb(bank)
                        rd = []
                        for s_ in range(nch):
                            rd += [WDb[s_]] + WD_OVER[s_] + [YTb[s_][i // 4]]
                        sc.pe_group([lambda e, s_=s_, i=i, hf=hf, ps=ps, nch=nch: e.matmul(
                            ps, lhsT=YT[:, s_, 128 * i:128 * i + 128], rhs=WDv[s_][:, 512 * hf:512 * hf + 512],
                            start=(s_ == 0), stop=(s_ == nch - 1)) for s_ in range(nch)],
                            reads=rd, writes=[PSb[bank]])
                        sc.op("dve", lambda e, i=i, hf=hf, ps=ps: e.tensor_tensor(
                            out=X[:, i, 512 * hf:512 * hf + 512], in0=X[:, i, 512 * hf:512 * hf + 512], in1=ps, op=ALU.add),
                            reads=[PSb[bank]], writes=[Xb[i][hf]])
                if lastsc:
                    ss_tile(NT - 1)

        def program():
            load_wqk(0, 0)
            load_wv(0, 0)
            for l in range(n_layers):
                emit_norm(2 * l, ss_done=(l > 0))
                if l == 0:
                    t5_prologue()
                if stop_after == "n1":
                    dump(HTt[:], HTb, 16384)
                    return
                fch = fox_layer_chunks(l, fbank=5)
                p0 = prep_chunks(l, 0)
                i0 = j0 = 0
                while i0 < len(p0) or j0 < len(fch):
                    for _ in range(2):
                        if i0 < len(p0):
                            p0[i0]()
                            i0 += 1
                    if j0 < len(fch):
                        fch[j0]()
                        j0 += 1
                fch = []
                heads = []
                chlists = []
                for idx in range(16):
                    heads.append(make_head(l, idx))
                for idx in range(16):
                    chs = []
                    if idx == 0:
                        chs += fch
                    if idx + 1 < 16:
                        chs += prep_chunks(l, idx + 1)
                    else:
                        def c_wo(l=l):
                            for c_ in range(8):
                                sc.dma("pool", WOv[:, 1024 * c_:1024 * c_ + 1024], wo_d[l, :, 1024 * c_:1024 * c_ + 1024],
                                       WOb, also_writes=WO_OVER)
                        chs.append(c_wo)
                    chlists.append(chs)
                nstop = 16
                if stop_after is not None and stop_after.startswith("m") and stop_after[1:].isdigit():
                    nstop = int(stop_after[1:]) + 1
                heads[0][1](0)
                for idx in range(nstop):
                    nT, qk, soft, pv = heads[idx]
                    chunks = chlists[idx]
                    nch = len(chunks)
                    done = 0
                    for n in range(nT):
                        soft(n)
                        tick_deferred()
                        want = (nch * (n + 1) + nT - 7) // (nT - 6)
                        if n + 1 < nT:
                            qk(n + 1)
                        else:
                            want = nch
                        while done < min(want, nch):
                            chunks[done]()
                            done += 1
                        if n + 1 == nT and idx + 1 < nstop:
                            heads[idx + 1][1](0)
                        pv(n)
                    assert done == nch
                if nstop < 16:
                    tick_deferred(flush=True)
                    dump(YTt[:], [b for bb in YTb for b in bb], 16384)
                    return
                tick_deferred(flush=True)
                if stop_after == "att":
                    dump(YTt[:], [b for bb in YTb for b in bb], 16384)
                    return
                outproj(l)
                if stop_after == "xatt":
                    sc.dma("sp", dbg_d, Xt[:], DBGb, reads=[b for bb in Xb for b in bb])
                    return
                load_gu(l, 0)
                ffn(l)
                if stop_after == "l0":
                    sc.dma("sp", dbg_d, Xt[:], DBGb, reads=[b for bb in Xb for b in bb])
                    return
            emit_norm(8, final=True, ss_done=True)

        program()
        finish()

        @block.tensor
        def _(e):
            sc.run("pe", e)

        @block.scalar
        def _(e):
            sc.run("act", e)

        @block.vector
        def _(e):
            sc.run("dve", e)

        @block.gpsimd
        def _(e):
            sc.run("pool", e)

        @block.sync
        def _(e):
            sc.run("sp", e)

    return nc


def _t5_bucket_np(dist):
    max_exact = 16
    d = np.maximum(dist, 1).astype(np.float32)
    large = max_exact + (np.log(d / max_exact) / math.log(1024 / max_exact) * (32 - max_exact)).astype(np.int32)
    large = np.minimum(large, 31)
    return np.where(dist < max_exact, dist, large)


def _constants():
    cst = np.zeros((128, 512), np.float32)
    cst[:, 0:128] = np.eye(128, dtype=np.float32)[::-1]
    s_ = np.arange(128)
    cst[:, 128:256] = (s_[:, None] <= s_[None, :]).astype(np.float32)
    pm = np.zeros((16, 8), np.float32)
    no = np.ones((16, 8), np.float32)
    for i in range(16):
        qb = i // 2
        pm[i, qb:] = -BIG
        no[i, qb] = 0.0
    cst[:, 256:384] = pm.reshape(1, 128)
    cst[:, 384:512] = no.reshape(1, 128)
    ident = np.eye(128, dtype=np.float32)
    kaug = np.zeros((2, 8, S), np.float32)
    for n in range(8):
        kaug[0, n, 256 * n:256 * n + 256] = 1.0
    kaug[1, 0:3, :] = 1.0
    ohb = np.zeros((33, EXTW), np.float32)
    u = np.arange(EXTW)
    dd = u - 127
    bk = _t5_bucket_np(np.maximum(dd, 0))
    for uu in range(EXTW):
        if dd[uu] >= 0:
            ohb[bk[uu], uu] = 1.0
        else:
            ohb[32, uu] = -BIG
    return cst, ident, kaug, ohb


def _prep_weights(w_in, b_f, w_o, g_attn, w_gu, w_down, g_ffn, rel_bias, g_final):
    f32 = np.float32
    w_in = np.asarray(w_in, f32)
    wk = w_in.reshape(L, 8, 128, 3080)
    wqk = np.empty((L, 16, 128, 8, 128), f32)
    wv = np.empty((L, 8, 128, 8, 128), f32)
    for typ in range(2):
        qo = 0 if typ == 0 else 1536
        ko = 512 if typ == 0 else 2048
        vo = 1024 if typ == 0 else 2560
        for hi in range(8):
            hh = hi + 8 * typ
            wqk[:, hh, :, :, 0:64] = wk[:, :, :, qo + 64 * hi:qo + 64 * hi + 64].transpose(0, 2, 1, 3)
            wqk[:, hh, :, :, 64:128] = wk[:, :, :, ko + 64 * hi:ko + 64 * hi + 64].transpose(0, 2, 1, 3)
        for pp in range(4):
            wv[:, pp + 4 * typ] = wk[:, :, :, vo + 128 * pp:vo + 128 * pp + 128].transpose(0, 2, 1, 3)
    wf = wk[:, :, :, 3072:3080].transpose(0, 2, 1, 3)
    wo = np.asarray(w_o, f32).reshape(L, 8, 128, 1024).transpose(0, 2, 1, 3)
    wgu = np.asarray(w_gu, f32).reshape(L, 8, 128, 2 * DFF)
    wg = wgu[:, :, :, 0:DFF].reshape(L, 8, 128, NFC, 128).transpose(0, 3, 2, 1, 4)
    wu = wgu[:, :, :, DFF:].reshape(L, 8, 128, NFC, 128).transpose(0, 3, 2, 1, 4)
    wd = np.asarray(w_down, f32).reshape(L * NFC, 128, 1024)
    gall = np.empty((9, D), f32)
    for l in range(L):
        gall[2 * l] = g_attn[l]
        gall[2 * l + 1] = g_ffn[l]
    gall[8] = g_final
    cst, ident, kaug, ohb = _constants()
    return {
        "wqk": np.ascontiguousarray(wqk.reshape(L * 16, 128, 1024)),
        "wv": np.ascontiguousarray(wv.reshape(L * 8, 128, 1024)),
        "wf": np.ascontiguousarray(wf.reshape(L, 128, 64)),
        "wo": np.ascontiguousarray(wo.reshape(L, 128, 8192)),
        "wg": np.ascontiguousarray(wg.reshape(L * NFC, 128, 1024)),
        "wu": np.ascontiguousarray(wu.reshape(L * NFC, 128, 1024)),
        "wd": np.ascontiguousarray(wd),
        "gall": gall,
        "bfl": np.ascontiguousarray(np.asarray(b_f, f32).reshape(1, 32)),
        "rb": np.ascontiguousarray(np.asarray(rel_bias, f32)),
        "cst": cst, "ident": ident, "kaug": kaug, "ohb": ohb,
    }


def kernel(x, w_in, b_f, w_o, g_attn, w_gu, w_down, g_ffn, rel_bias, g_final):
    x = np.asarray(x, np.float32)
    shared = _prep_weights(w_in, b_f, w_o, g_attn, w_gu, w_down, g_ffn, rel_bias, g_final)
    nc = build_program()
    in_maps = []
    for c in range(N_CORES):
        m = dict(shared)
        m["x"] = np.ascontiguousarray(x[c])
        in_maps.append(m)
    res = run_bass_kernel_spmd(nc, in_maps, core_ids=list(range(N_CORES)))
    return np.stack([np.asarray(r["out"], np.float32) for r in res.results], axis=0)
```
